# Optimizing a Trainium2 kernel written in Bass

```python
import math
import jax, jax.numpy as jnp
from jax import lax
import numpy as np

D_MODEL = 1024
BATCH = 32
SEQ = 2048
DEPTH = 1

EPS = 1e-6
NEG = -1e30
D_FF = 2816

MLA_HEADS = 8
Q_LORA = 256
KV_LORA = 128
QK_NOPE = 64
QK_ROPE = 32
V_HEAD = 64
ROPE_THETA = 10000.0
Q_BLOCK = 128

SWA_HEADS = 8
SWA_KV_HEADS = 2
SWA_HEAD_DIM = 64
WINDOW = 128

REL_BUCKETS = 32
REL_MAX_DIST = 128

MLA_OUT = MLA_HEADS * V_HEAD
SWA_OUT = SWA_HEADS * SWA_HEAD_DIM
D_MIX = MLA_OUT + SWA_OUT
IN_WIDTHS = (Q_LORA, KV_LORA + QK_ROPE, SWA_HEADS * SWA_HEAD_DIM,
             SWA_KV_HEADS * SWA_HEAD_DIM, SWA_KV_HEADS * SWA_HEAD_DIM)
D_IN = sum(IN_WIDTHS)
IN_SPLITS = tuple(int(s) for s in np.cumsum(IN_WIDTHS)[:-1])

kernel_name = "hymba_mla_swa_macaron_t5"


def rmsnorm(x, g):
    xf = x.astype(jnp.float32)
    y = xf * lax.rsqrt(jnp.mean(xf * xf, axis=-1, keepdims=True) + EPS)
    return (y * g.astype(jnp.float32)).astype(x.dtype)


def swiglu(x, w_gate, w_up, w_down):
    return (jax.nn.silu(x @ w_gate) * (x @ w_up)) @ w_down


def rope(x, cos, sin):
    half = x.shape[-1] // 2
    xf = x.astype(jnp.float32)
    x1, x2 = xf[..., :half], xf[..., half:]
    return jnp.concatenate([x1 * cos - x2 * sin, x2 * cos + x1 * sin], axis=-1).astype(x.dtype)


def t5_bucket(dist):
    n = jnp.maximum(dist, 0)
    max_exact = REL_BUCKETS // 2
    nf = jnp.maximum(n, 1).astype(jnp.float32)
    large = max_exact + (jnp.log(nf / max_exact) / math.log(REL_MAX_DIST / max_exact)
                         * (REL_BUCKETS - max_exact)).astype(jnp.int32)
    large = jnp.minimum(large, REL_BUCKETS - 1)
    return jnp.where(n < max_exact, n, large)


def mla_group(c_q, ckv_pe, g_q_a, w_q_b, g_kv_a, w_kv_b):
    B, S, _ = c_q.shape
    pos = jnp.arange(S, dtype=jnp.float32)
    inv_freq = ROPE_THETA ** (-jnp.arange(0, QK_ROPE, 2, dtype=jnp.float32) / QK_ROPE)
    ang = pos[:, None] * inv_freq[None, :]
    cos, sin = jnp.cos(ang), jnp.sin(ang)

    q = (rmsnorm(c_q, g_q_a) @ w_q_b).reshape(B, S, MLA_HEADS, QK_NOPE + QK_ROPE)
    q_nope = q[..., :QK_NOPE]
    q_pe = rope(q[..., QK_NOPE:], cos[:, None, :], sin[:, None, :])
    c_kv = ckv_pe[..., :KV_LORA]
    k_pe = rope(ckv_pe[..., KV_LORA:], cos, sin)
    kv = (rmsnorm(c_kv, g_kv_a) @ w_kv_b).reshape(B, S, MLA_HEADS, QK_NOPE + V_HEAD)
    k_nope, v = kv[..., :QK_NOPE], kv[..., QK_NOPE:]
    scale = (QK_NOPE + QK_ROPE) ** -0.5

    nb = S // Q_BLOCK
    qn_b = q_nope.reshape(B, nb, Q_BLOCK, MLA_HEADS, QK_NOPE).transpose(1, 0, 2, 3, 4)
    qp_b = q_pe.reshape(B, nb, Q_BLOCK, MLA_HEADS, QK_ROPE).transpose(1, 0, 2, 3, 4)
    kpos = jnp.arange(S)

    def block(args):
        qn, qp, i = args
        s = (jnp.einsum('bqhd,bkhd->bhqk', qn, k_nope)
             + jnp.einsum('bqhr,bkr->bhqk', qp, k_pe)).astype(jnp.float32) * scale
        qpos = i * Q_BLOCK + jnp.arange(Q_BLOCK)
        causal = kpos[None, :] <= qpos[:, None]
        s = jnp.where(causal[None, None], s, NEG)
        p = jax.nn.softmax(s, axis=-1).astype(v.dtype)
        return jnp.einsum('bhqk,bkhd->bqhd', p, v)

    o = lax.map(block, (qn_b, qp_b, jnp.arange(nb)))
    return o.transpose(1, 0, 2, 3, 4).reshape(B, S, MLA_OUT)


def swa_group(q, k, v, sinks, rel_bias):
    B, S, _ = q.shape
    nb = S // WINDOW
    G = SWA_HEADS // SWA_KV_HEADS
    dh = SWA_HEAD_DIM
    q = q.reshape(B, nb, WINDOW, SWA_KV_HEADS, G, dh)
    k = k.reshape(B, nb, WINDOW, SWA_KV_HEADS, dh)
    v = v.reshape(B, nb, WINDOW, SWA_KV_HEADS, dh)

    def band(t):
        prev = jnp.pad(t, ((0, 0), (1, 0), (0, 0), (0, 0), (0, 0)))[:, :-1]
        return jnp.concatenate([prev, t], axis=2)

    kb, vb = band(k), band(v)
    s = jnp.einsum('bnqhgd,bnkhd->bnhgqk', q, kb).astype(jnp.float32) * (dh ** -0.5)

    qi = jnp.arange(WINDOW)[:, None]
    kj = jnp.arange(2 * WINDOW)[None, :]
    dist = qi + WINDOW - kj
    bias = rel_bias[t5_bucket(dist)]
    bias = bias.transpose(2, 0, 1).reshape(SWA_KV_HEADS, G, WINDOW, 2 * WINDOW)
    kpos = jnp.arange(nb)[:, None, None] * WINDOW - WINDOW + kj[None]
    valid = (dist >= 0)[None] & (dist < WINDOW)[None] & (kpos >= 0)

    s = s + bias.astype(jnp.float32)[None, None]
    s = jnp.where(valid[None, :, None, None], s, NEG)
    sink = sinks.astype(jnp.float32).reshape(SWA_KV_HEADS, G)[None, None, :, :, None, None]
    m = jnp.maximum(jnp.max(s, axis=-1, keepdims=True), sink)
    p = jnp.exp(s - m)
    denom = jnp.sum(p, axis=-1, keepdims=True) + jnp.exp(sink - m)
    o = jnp.einsum('bnhgqk,bnkhd->bnqhgd', (p / denom).astype(vb.dtype), vb)
    return o.reshape(B, S, SWA_OUT)


def setup_inputs(seed: int = 0) -> dict:
    key = jax.random.key(seed)
    ks = iter(jax.random.split(key, 32))
    L = DEPTH

    def w(shape, fan_in):
        return jax.random.normal(next(ks), shape, jnp.float32) * fan_in ** -0.5

    def gain(shape):
        return 1.0 + 0.02 * jax.random.normal(next(ks), shape, jnp.float32)

    return {
        "x": jax.random.normal(next(ks), (BATCH, SEQ, D_MODEL), jnp.float32),
        "g_ffn1": gain((L, D_MODEL)),
        "w_ffn1_gate": w((L, D_MODEL, D_FF), D_MODEL),
        "w_ffn1_up": w((L, D_MODEL, D_FF), D_MODEL),
        "w_ffn1_down": w((L, D_FF, D_MODEL), D_FF),
        "g_mix": gain((L, D_MODEL)),
        "w_in": w((L, D_MODEL, D_IN), D_MODEL),
        "g_q_a": gain((L, Q_LORA)),
        "w_q_b": w((L, Q_LORA, MLA_HEADS * (QK_NOPE + QK_ROPE)), Q_LORA),
        "g_kv_a": gain((L, KV_LORA)),
        "w_kv_b": w((L, KV_LORA, MLA_HEADS * (QK_NOPE + V_HEAD)), KV_LORA),
        "attn_sinks": 0.5 * jax.random.normal(next(ks), (L, SWA_HEADS), jnp.float32),
        "rel_bias": 0.5 * jax.random.normal(next(ks), (REL_BUCKETS, SWA_HEADS), jnp.float32),
        "g_out_mla": gain((L, MLA_OUT)),
        "g_out_swa": gain((L, SWA_OUT)),
        "w_o": w((L, D_MIX, D_MODEL), D_MIX),
        "g_ffn2": gain((L, D_MODEL)),
        "w_ffn2_gate": w((L, D_MODEL, D_FF), D_MODEL),
        "w_ffn2_up": w((L, D_MODEL, D_FF), D_MODEL),
        "w_ffn2_down": w((L, D_FF, D_MODEL), D_FF),
        "g_final": gain((D_MODEL,)),
    }


def reference(x, g_ffn1, w_ffn1_gate, w_ffn1_up, w_ffn1_down, g_mix, w_in, g_q_a, w_q_b,
              g_kv_a, w_kv_b, attn_sinks, rel_bias, g_out_mla, g_out_swa, w_o, g_ffn2,
              w_ffn2_gate, w_ffn2_up, w_ffn2_down, g_final):
    h = x
    for l in range(DEPTH):
        h = h + 0.5 * swiglu(rmsnorm(h, g_ffn1[l]), w_ffn1_gate[l], w_ffn1_up[l], w_ffn1_down[l])
        u = rmsnorm(h, g_mix[l])
        proj = u @ w_in[l]
        c_q, ckv_pe, q_s, k_s, v_s = jnp.split(proj, IN_SPLITS, axis=-1)
        o_mla = mla_group(c_q, ckv_pe, g_q_a[l], w_q_b[l], g_kv_a[l], w_kv_b[l])
        o_swa = swa_group(q_s, k_s, v_s, attn_sinks[l], rel_bias)
        o = jnp.concatenate([rmsnorm(o_mla, g_out_mla[l]), rmsnorm(o_swa, g_out_swa[l])], axis=-1)
        h = h + o @ w_o[l]
        h = h + 0.5 * swiglu(rmsnorm(h, g_ffn2[l]), w_ffn2_gate[l], w_ffn2_up[l], w_ffn2_down[l])
    return rmsnorm(h, g_final)
```

```python
import contextlib
import math

import numpy as np
import concourse.bass as bass
import concourse.mybir as mybir
from concourse.bass_utils import run_bass_kernel_spmd

F32 = mybir.dt.float32
BF16 = mybir.dt.bfloat16
AF = mybir.ActivationFunctionType
ALU = mybir.AluOpType

D = 1024
S = 2048
DFF = 2816
NFC = DFF // 128
T = 512
NTI = S // T
EPS = 1e-6
NEG = -30000.0
SC_MLA = 96.0 ** -0.5
SC_SWA = 0.125
WIN_COLS = 448 + 512 + 256
ASLOT = 4096
BSLOT = 1536
HCH = 12
NA = 3
NB = 3


class Tracker:
    def __init__(self, nc, es):
        self.nc = nc
        self.engs = {"pe": nc.tensor, "act": nc.scalar, "dve": nc.vector, "pool": nc.gpsimd, "sp": nc.sync}
        self.sems = {}
        self.cnt = {}
        for e in ("pe", "act", "dve", "pool"):
            self.sems[e] = es.enter_context(nc.semaphore("sem_" + e))
            self.cnt[e] = 0
        self.es = es
        self.seen = {e: {} for e in self.engs}
        self.last_w = {}
        self.readers = {}
        self.dma_sems = {}
        self.nwaits = 0

    def dma_sem(self, name):
        if name not in self.dma_sems:
            self.dma_sems[name] = [self.es.enter_context(self.nc.semaphore("dsem_" + name)), 0]
        return self.dma_sems[name]

    def _wait(self, e, tok):
        name, sem, val = tok
        if name == "pe" and e == "pe":
            return
        if self.seen[e].get(name, 0) >= val:
            return
        self.engs[e].wait_ge(sem, val)
        self.nwaits += 1
        self.seen[e][name] = val

    def _deps(self, e, reads, writes):
        toks = {}

        def add(tok):
            if tok is None:
                return
            if toks.get(tok[0], (None, None, -1))[2] < tok[2]:
                toks[tok[0]] = tok

        for k in reads:
            add(self.last_w.get(k))
        for k in writes:
            add(self.last_w.get(k))
            for t in self.readers.get(k, {}).values():
                add(t)
        for tok in toks.values():
            self._wait(e, tok)

    def _commit(self, tok, reads, writes):
        for k in reads:
            self.readers.setdefault(k, {})[tok[0]] = tok
        for k in writes:
            self.last_w[k] = tok
            self.readers[k] = {}

    def op(self, e, reads, writes, fn):
        self._deps(e, reads, writes)
        inst = fn()
        self.cnt[e] += 1
        inst.then_inc(self.sems[e], 1)
        tok = (e, self.sems[e], self.cnt[e])
        self._commit(tok, reads, writes)
        return tok

    def dma(self, q, semname, reads, writes, out, in_, multi=False):
        self._deps(q, reads, writes)
        s = self.dma_sem(semname)
        if not multi and s[1] > 0:
            self._wait(q, ("d_" + semname, s[0], s[1]))
        inst = self.engs[q].dma_start(out=out, in_=in_)
        s[1] += 16
        inst.then_inc(s[0], 16)
        tok = ("d_" + semname, s[0], s[1])
        self._commit(tok, reads, writes)
        return tok

    def barrier(self):
        toks = []
        for e in ("pe", "act", "dve", "pool"):
            if self.cnt[e] > 0:
                toks.append((e, self.sems[e], self.cnt[e]))
        for name, (sem, val) in self.dma_sems.items():
            if val > 0 and not name.startswith("cv"):
                toks.append(("d_" + name, sem, val))
        for e in self.engs:
            for tok in toks:
                if tok[0] == "pe" and e == "pe":
                    continue
                self._wait(e, tok)
        self.last_w = {k: v for k, v in self.last_w.items() if k[0] == "scr"}
        self.readers = {}

    def family(self, semname, keys):
        s = self.dma_sems[semname]
        tok = ("d_" + semname, s[0], s[1])
        for k in keys:
            self.last_w[k] = tok


class Stream:
    def __init__(self, tr, name, slots, items):
        self.tr = tr
        self.name = name
        self.slots = slots
        self.items = items
        self.n = len(slots)
        self.issued = 0
        self.cur = 0

    def _issue(self):
        k = self.issued
        if k >= len(self.items):
            return
        src, n, key = self.items[k]
        s = k % self.n
        self.tr.dma("sp", f"{self.name}{s}", [key], [(self.name, s)], self.slots[s][:, 0:n], src)
        self.issued += 1

    def start(self):
        for _ in range(self.n):
            self._issue()

    def get(self):
        k = self.cur
        assert k < self.issued
        s = k % self.n
        return self.slots[s], (self.name, s)

    def release(self):
        self.cur += 1
        self._issue()


def build_program(NSEQ, debug=False):
    nc = bass.Bass("TRN2", target_bir_lowering=False)

    def din(name, shape, dt=F32):
        return nc.dram_tensor(name, list(shape), dt, kind="ExternalInput").ap()

    def dscr(name, shape, dt=BF16):
        return nc.dram_tensor(name, list(shape), dt, kind="Internal").ap()

    xT = din("xT", [NSEQ, D, S])
    outT = nc.dram_tensor("outT", [NSEQ, D, S], F32, kind="ExternalOutput").ap()
    wgu_in = [din("wgu1", [11, 128, ASLOT]), din("wgu2", [11, 128, ASLOT])]
    wd_in = [din("wd1", [16, 128, BSLOT]), din("wd2", [16, 128, BSLOT])]
    win_in = din("win", [128, 8 * WIN_COLS])
    wqb_in = din("wqb", [128, 2 * 8 * 128])
    wkvb_in = din("wkvb", [128, 1024])
    wo_in = din("wo", [2, 128, ASLOT])
    gains_in = din("gains", [128, 35])
    gout_in = din("gout", [1, 1024])
    sinks_in = din("sinks", [1, 8])
    relb_in = din("relb", [1, 256])
    cst_in = din("cst", [128, 4 * 128])
    mb_in = din("mb", [128, 32 * 256])
    rope_in = din("rope", [NTI, 128, 2 * T])

    wgu_s = [dscr("wgu1s", [11, 128, ASLOT]), dscr("wgu2s", [11, 128, ASLOT])]
    wd_s = [dscr("wd1s", [16, 128, BSLOT]), dscr("wd2s", [16, 128, BSLOT])]
    win_s = dscr("wins", [128, 8 * WIN_COLS])
    wqb_s = dscr("wqbs", [128, 2 * 8 * 128])
    wkvb_s = dscr("wkvbs", [128, 1024])
    wo_s = dscr("wos", [2, 128, ASLOT])

    dbg = {}
    with contextlib.ExitStack() as es:
        es.enter_context(nc.allow_low_precision("bf16 matmul operands, fp32 accumulation"))
        tr = Tracker(nc, es)

        def sb(name, shape, dt):
            return es.enter_context(nc.sbuf_tensor("sb_" + name, list(shape), dt))

        gains = sb("gains", [128, 35], F32)
        gout = sb("gout", [128, 1024], F32)
        esink = sb("esink", [128, 8], F32)
        ident = sb("ident", [128, 128], BF16)
        cmask = sb("cmask", [128, 128], BF16)
        Bhi = sb("Bhi", [128, 2, 8, 128], BF16)
        Blo = sb("Blo", [128, 2, 8, 128], BF16)
        onesD = sb("onesD", [128, 128], BF16)
        onesQ = sb("onesQ", [128, 128], BF16)
        onesK = sb("onesK", [128, 128], BF16)
        epsb = sb("epsb", [128, 1], F32)

        PS = [es.enter_context(nc.psum_tensor(f"ps{i}", [128, 512], F32)) for i in range(7)]
        PTR = es.enter_context(nc.psum_tensor("ptr", [128, 1024], BF16))

        def P(b):
            return ("P", b)

        hTs = [sb("hT0", [128, 8, T], F32), sb("hT1", [128, 8, T], F32)]
        tr.dma("sp", "xinS", [], [("h", 0, kc) for kc in range(8)], hTs[0][:],
               xT[0, :, 0:T].rearrange("(kc p) t -> p kc t", p=128))
        for g in range(11):
            tr.dma("pool", f"cva{g}", [], [("scr", "wgu0", g)], wgu_s[0][g], wgu_in[0][g])
        tr.dma("sp", "cst", [], [("gains",)], gains[:], gains_in, multi=True)
        tr.dma("sp", "cst", [], [("gout",)], gout[:], gout_in.partition_broadcast(128), multi=True)
        tr.dma("sp", "cst", [], [("esink",)], esink[:], sinks_in.partition_broadcast(128), multi=True)
        tr.family("cst", [("gains",), ("gout",), ("esink",)])
        tr.dma("pool", "cstc", [], [("ident",)], ident[:], cst_in[:, 0:128], multi=True)
        tr.dma("pool", "cstc", [], [("cmask",)], cmask[:], cst_in[:, 128:256], multi=True)
        tr.family("cstc", [("ident",), ("cmask",)])
        for c in range(16):
            tr.dma("pool", "cvb", [], [("scr", "wd0", c)], wd_s[0][c], wd_in[0][c], multi=True)
        tr.family("cvb", [("scr", "wd0", c) for c in range(16)])
        tr.dma("pool", "cvm", [], [("scr", "win")], win_s, win_in, multi=True)
        tr.dma("pool", "cvm", [], [("scr", "wqb")], wqb_s, wqb_in, multi=True)
        tr.dma("pool", "cvm", [], [("scr", "wkvb")], wkvb_s, wkvb_in, multi=True)
        for c in range(2):
            tr.dma("pool", "cvm", [], [("scr", "wo", c)], wo_s[c], wo_in[c], multi=True)
        tr.family("cvm", [("scr", "win"), ("scr", "wqb"), ("scr", "wkvb"), ("scr", "wo", 0), ("scr", "wo", 1)])
        for g in range(11):
            tr.dma("pool", "cvc", [], [("scr", "wgu1", g)], wgu_s[1][g], wgu_in[1][g], multi=True)
        tr.family("cvc", [("scr", "wgu1", g) for g in range(11)])
        for c in range(16):
            tr.dma("pool", "cvd", [], [("scr", "wd1", c)], wd_s[1][c], wd_in[1][c], multi=True)
        tr.family("cvd", [("scr", "wd1", c) for c in range(16)])

        tr.op("dve", [], [("onesD",)], lambda: nc.vector.memset(onesD[:], 1.0 / 1024))
        tr.op("dve", [], [("onesQ",)], lambda: nc.vector.memset(onesQ[:], 1.0 / 256))
        tr.op("dve", [], [("onesK",)], lambda: nc.vector.memset(onesK[:], 1.0 / 128))
        tr.op("dve", [], [("epsb",)], lambda: nc.vector.memset(epsb[:], EPS))
        tr.op("act", [("esink",)], [("esink",)],
              lambda: nc.scalar.activation(out=esink[:], in_=esink[:], func=AF.Exp))

        KT = sb("KT", [128, 8, S], BF16)
        VA = sb("VA", [128, 16, 8, 65], BF16)
        ksT = sb("ksT", [128, 2, S], BF16)
        VS = sb("VS", [128, 16, 2, 65], BF16)
        rope = sb("rope", [128, 2, T], F32)
        u = sb("u", [128, 8, T], BF16)
        sq = sb("sq", [128, 2, T], BF16)
        rstd = sb("rstd", [128, T], F32)
        rstdN = sb("rstdN", [128, T], F32)
        act = sb("act", [128, HCH, T], BF16)
        sg = sb("sg", [128, 2, T], F32)
        slotA = [sb(f"slotA{i}", [128, ASLOT], BF16) for i in range(NA)]
        slotB = [sb(f"slotB{i}", [128, BSLOT], BF16) for i in range(NB)]
        cqn = sb("cqn", [128, 2, T], BF16)
        ckvn = sb("ckvn", [128, T], BF16)
        kper = sb("kper", [128, T], BF16)
        QT = sb("QT", [128, 8, T], BF16)
        PT = act
        om = sb("om", [128, 4, 512], F32)
        osw = sb("osw", [128, 2, 512], F32)
        onb = sb("onb", [128, 2, 1024], BF16)
        rden = sb("rden", [128, 2, 4], F32)
        den = sb("den", [128, 2, 4], F32)
        ssq = sb("ssq", [128, 2, 2], F32)
        rs2 = sb("rs2", [128, 2, 2], F32)
        tr.op("dve", [], [("VA", kb) for kb in range(16)], lambda: nc.vector.memset(VA[:], 1.0))
        tr.op("dve", [], [("VS", kb) for kb in range(16)], lambda: nc.vector.memset(VS[:], 1.0))

        mbt = KT[:].rearrange("p h s -> p (h s)").bitcast(F32).rearrange("p (b w q) -> p b w q", b=32, w=2)
        Bf = om[:].rearrange("p a b -> p (a b)").rearrange("p (w h q) -> p w h q", w=2, h=8)
        rb = rstdN[:, 0:256]
        mi = rstdN[:, 256:512].rearrange("p (w q) -> p w q", w=2)
        tr.dma("sp", "cst2", [], [("mbt",)], KT[:].rearrange("p h s -> p (h s)").bitcast(F32), mb_in, multi=True)
        tr.dma("sp", "cst2", [], [("rb",)], rb, relb_in.partition_broadcast(128), multi=True)
        tr.dma("sp", "cst2", [], [("mi",)], rstdN[:, 256:512], cst_in[:, 256:512], multi=True)
        tr.family("cst2", [("mbt",), ("rb",), ("mi",)])
        tbl_ops = []
        tbl_ops.append(([("rb",)], [("rb",)],
                        lambda: nc.vector.tensor_scalar(out=rb, in0=rb, scalar1=8.0, scalar2=None, op0=ALU.mult)))
        for hd in range(8):
            tbl_ops.append(([("mi",)], [("Bf", hd)],
                            lambda hd=hd: nc.vector.tensor_copy(out=Bf[:, :, hd, :], in_=mi)))
        for b in range(32):
            for hd in range(8):
                tbl_ops.append(([("mbt",), ("rb",), ("Bf", hd)], [("Bf", hd)],
                                lambda b=b, hd=hd: nc.vector.scalar_tensor_tensor(
                                    out=Bf[:, :, hd, :], in0=mbt[:, b, :, :], scalar=rb[:, b * 8 + hd:b * 8 + hd + 1],
                                    in1=Bf[:, :, hd, :], op0=ALU.mult, op1=ALU.add)))
        allBf = [("Bf", hd) for hd in range(8)]
        tbl_ops.append((allBf, [("Bhi",)], lambda: nc.vector.tensor_copy(out=Bhi[:], in_=Bf)))
        tbl_ops.append((allBf + [("Bhi",)], [("Blo",)],
                        lambda: nc.vector.tensor_tensor(out=Blo[:], in0=Bf, in1=Bhi[:], op=ALU.subtract)))
        tbl_last = [None]

        def tbl_filler(k):
            for _ in range(k):
                if not tbl_ops:
                    return
                r, w, fn = tbl_ops.pop(0)
                tbl_last[0] = tr.op("dve", r, w, fn)

        itemsA = []
        itemsB = []
        for si in range(NSEQ):
            for ti in range(NTI):
                for g in range(11):
                    itemsA.append((wgu_s[0][g], ASLOT, ("scr", "wgu0", g)))
                itemsA.append((win_s[:, 0:8 * 448], 8 * 448, ("scr", "win")))
                itemsA.append((win_s[:, 8 * 960:8 * 1216], 8 * 256, ("scr", "win")))
                itemsA.append((wqb_s, 2048, ("scr", "wqb")))
                itemsA.append((wkvb_s, 1024, ("scr", "wkvb")))
                itemsA.append((win_s[:, 8 * 448:8 * 960], 8 * 512, ("scr", "win")))
                itemsA.append((wo_s[0], ASLOT, ("scr", "wo", 0)))
                itemsA.append((wo_s[1], ASLOT, ("scr", "wo", 1)))
                for g in range(11):
                    itemsA.append((wgu_s[1][g], ASLOT, ("scr", "wgu1", g)))
                for f in range(2):
                    for hf in range(2):
                        for c in range(8):
                            itemsB.append((wd_s[f][hf * 8 + c], BSLOT, ("scr", f"wd{f}", hf * 8 + c)))
        stA = Stream(tr, "A", slotA, itemsA)
        stB = Stream(tr, "B", slotB, itemsB)

        def HKn(n):
            return [("h", n % 2, kc) for kc in range(8)]

        def load_x(n):
            si, ti = divmod(n, NTI)
            tr.dma("pool", f"xin{n % 2}", [], HKn(n), hTs[n % 2][:],
                   xT[si, :, ti * T:(ti + 1) * T].rearrange("(kc p) t -> p kc t", p=128))

        stA.start()
        stB.start()

        bank_rr = [0]

        def next_bank(pool=(0, 1, 2, 3)):
            b = pool[bank_rr[0] % len(pool)]
            bank_rr[0] += 1
            return b

        UK = [("u", kc) for kc in range(8)]

        def stats_finish(bank, rs, rskey):
            tr.op("act", [P(bank), ("epsb",)], [rskey],
                  lambda: nc.scalar.activation(out=rs[:], in_=PS[bank][:], func=AF.Ln, bias=epsb[:], scale=1.0))
            tr.op("act", [rskey], [rskey],
                  lambda: nc.scalar.activation(out=rs[:], in_=rs[:], func=AF.Exp, scale=-0.5))

        def rms_stats(src_sq_fn, nchunks, ones, oneskey, bank=6, rs=None, rskey=("rstd",)):
            rs = rstd if rs is None else rs
            for c in range(nchunks):
                s = c % 2
                src_sq_fn(c, s)
                tr.op("pe", [("sq", s), oneskey], [P(bank)],
                      lambda c=c, s=s: nc.tensor.matmul(PS[bank][:], lhsT=ones[:], rhs=sq[:, s, :],
                                                        start=(c == 0), stop=(c == nchunks - 1)))
            stats_finish(bank, rs, rskey)

        def h_stats(n, bank=6, rs=None, rskey=("rstd",)):
            hcur = hTs[n % 2]

            def sqfn(c, s):
                tr.op("act", [("h", n % 2, c)], [("sq", s)],
                      lambda: nc.scalar.activation(out=sq[:, s, :], in_=hcur[:, c, :], func=AF.Square))
            rms_stats(sqfn, 8, onesD, ("onesD",), bank, rs, rskey)

        class ResidStats:
            def __init__(self, n, bank=6, rs=None, rskey=("rstd",)):
                self.n = n
                self.bank = bank
                self.rs = rstd if rs is None else rs
                self.rskey = rskey
                self.pending = None

            def _mm(self, c, last):
                s = c % 2
                bank = self.bank
                tr.op("pe", [("sq", s), ("onesD",)], [P(bank)],
                      lambda: nc.tensor.matmul(PS[bank][:], lhsT=onesD[:], rhs=sq[:, s, :], start=(c == 0),
                                               stop=last))

            def chunk_done(self, dc):
                hcur = hTs[self.n % 2]
                s = dc % 2
                tr.op("act", [("h", self.n % 2, dc)], [("sq", s)],
                      lambda: nc.scalar.activation(out=sq[:, s, :], in_=hcur[:, dc, :], func=AF.Square))
                self.pending = dc

            def after_pe_group(self):
                if self.pending is not None and self.pending < 7:
                    self._mm(self.pending, False)
                    self.pending = None

            def finish(self):
                self._mm(7, True)
                stats_finish(self.bank, self.rs, self.rskey)

        def pe_k_ops(bank, M, lhs_fn, rhs_fn, rkeys_fn, nk=8):
            for kc in range(nk):
                tr.op("pe", rkeys_fn(kc), [P(bank)],
                      lambda kc=kc: nc.tensor.matmul(PS[bank][0:M, :], lhsT=lhs_fn(kc), rhs=rhs_fn(kc),
                                                     start=(kc == 0), stop=(kc == nk - 1)))

        def make_u(n, gcol, rs=None, rskey=("rstd",)):
            rs = rstd if rs is None else rs
            hcur = hTs[n % 2]
            for kc in range(8):
                tr.op("dve", [("h", n % 2, kc), rskey, ("gains",)], [("u", kc)],
                      lambda kc=kc: nc.vector.scalar_tensor_tensor(
                          out=u[:, kc, :], in0=hcur[:, kc, :], scalar=gains[:, gcol + kc:gcol + kc + 1],
                          in1=rs[:], op0=ALU.mult, op1=ALU.mult))

        def ffn(n, next_n=None, filler=None):
            hcur = hTs[n % 2]
            rst = ResidStats(n)
            nst = ResidStats(next_n, 6, rstdN, ("rstdN",)) if next_n is not None else None
            halves = [(0, 12), (12, 10)]
            for hf, (c_lo, nch) in enumerate(halves):
                for g in range(nch // 2):
                    slot, skey = stA.get()
                    w = slot[:].rearrange("p (a k f) -> p a k f", a=2, k=8)
                    for c in range(2):
                        lc = 2 * g + c
                        gb = (0, 1)[lc % 2]
                        ub = (2, 3)[lc % 2]

                        def mm(a, bank):
                            inst = None
                            for kc in range(8):
                                inst = nc.tensor.matmul(PS[bank][:], lhsT=w[:, a, kc, c * 128:(c + 1) * 128],
                                                        rhs=u[:, kc, :], start=(kc == 0), stop=(kc == 7))
                            return inst
                        if hf == 0 and lc == 0:
                            pe_k_ops(gb, 128, lambda kc: w[:, 0, kc, c * 128:(c + 1) * 128], lambda kc: u[:, kc, :],
                                     lambda kc: [("u", kc), skey])
                        else:
                            tr.op("pe", UK + [skey], [P(gb)], lambda: mm(0, gb))
                        tr.op("pe", UK + [skey], [P(ub)], lambda: mm(1, ub))
                        s = lc % 2
                        tr.op("act", [P(gb)], [("sg", s)],
                              lambda: nc.scalar.activation(out=sg[:, s, :], in_=PS[gb][:], func=AF.Silu))
                        tr.op("dve", [("sg", s), P(ub)], [("act", lc)],
                              lambda: nc.vector.tensor_tensor(out=act[:, lc, :], in0=sg[:, s, :], in1=PS[ub][:],
                                                              op=ALU.mult))
                        if filler is not None:
                            filler(6)
                    stA.release()
                if hf == 1 and nst is not None:
                    make_u(next_n, 0, rs=rstdN, rskey=("rstdN",))
                AK = [("act", lc) for lc in range(nch)]
                for dc in range(8):
                    slot, skey = stB.get()
                    w = slot[:, 0:nch * 128].rearrange("p (f d) -> p f d", f=nch)
                    bank = (4, 5)[dc % 2]

                    def mm(lo=0, hi=nch):
                        inst = None
                        for lc in range(lo, hi):
                            inst = nc.tensor.matmul(PS[bank][:], lhsT=w[:, lc, :], rhs=act[:, lc, :],
                                                    start=(lc == 0), stop=(lc == nch - 1))
                        return inst
                    if dc == 0:
                        tr.op("pe", AK[:nch - 2] + [skey], [P(bank)], lambda: mm(0, nch - 2))
                        tr.op("pe", AK[nch - 2:] + [skey], [P(bank)], lambda: mm(nch - 2, nch))
                    else:
                        tr.op("pe", AK + [skey], [P(bank)], mm)
                    stB.release()
                    if hf == 1:
                        rst.after_pe_group()
                    elif nst is not None:
                        nst.after_pe_group()
                    tr.op("dve", [P(bank), ("h", n % 2, dc)], [("h", n % 2, dc)],
                          lambda: nc.vector.scalar_tensor_tensor(out=hcur[:, dc, :], in0=PS[bank][:], scalar=0.5,
                                                                 in1=hcur[:, dc, :], op0=ALU.mult, op1=ALU.add))
                    if hf == 1:
                        rst.chunk_done(dc)
                    elif nst is not None:
                        nst.chunk_done(dc)
                    if filler is not None:
                        filler(4)
                if hf == 0 and nst is not None:
                    nst.finish()
            rst.finish()

        def mixer(n):
            si, ti = divmod(n, NTI)
            hcur = hTs[n % 2]
            c0 = ti * T
            t1 = sg[:, 0, :]
            t2 = sg[:, 1, :]
            T1K, T2K = ("sg", 0), ("sg", 1)
            tr.dma("sp", "rope", [], [("rope",)], rope[:].rearrange("p a t -> p (a t)"), rope_in[ti])
            make_u(n, 8)
            PB = (0, 1, 2, 3, 4, 5)
            slot, skey = stA.get()
            w0 = slot[:, 0:8 * 448].rearrange("p (k f) -> p k f", k=8)

            def proj(w, cols, M, bank):
                inst = None
                for kc in range(8):
                    inst = nc.tensor.matmul(PS[bank][0:M, :], lhsT=w[:, kc, cols[0]:cols[1]], rhs=u[:, kc, :],
                                            start=(kc == 0), stop=(kc == 7))
                return inst
            bq = [0, 1]
            pe_k_ops(bq[0], 128, lambda kc: w0[:, kc, 0:128], lambda kc: u[:, kc, :], lambda kc: [("u", kc), skey])
            tr.op("pe", UK + [skey], [P(bq[1])], lambda: proj(w0, (128, 256), 128, bq[1]))
            bkv = 2
            tr.op("pe", UK + [skey], [P(bkv)], lambda: proj(w0, (256, 384), 128, bkv))
            bka = 3
            tr.op("pe", UK + [skey], [P(bka)], lambda: proj(w0, (384, 416), 32, bka))
            bkb = 4
            tr.op("pe", UK + [skey], [P(bkb)], lambda: proj(w0, (416, 448), 32, bkb))
            stA.release()
            bank_rr[0] = 5
            tr.op("dve", [P(bka), ("rope",)], [T1K],
                  lambda: nc.vector.tensor_tensor(out=t1[0:32, :], in0=PS[bka][0:32, :], in1=rope[0:32, 0, :],
                                                  op=ALU.mult))
            tr.op("dve", [P(bkb), ("rope",)], [T2K],
                  lambda: nc.vector.tensor_tensor(out=t2[0:32, :], in0=PS[bkb][0:32, :], in1=rope[0:32, 1, :],
                                                  op=ALU.mult))
            tr.op("dve", [T1K, T2K], [("kper",)],
                  lambda: nc.vector.tensor_tensor(out=kper[0:32, :], in0=t1[0:32, :], in1=t2[0:32, :], op=ALU.add))
            for hd in range(8):
                tr.op("pool", [("kper",)], [("KTpe", hd, ti)],
                      lambda hd=hd: nc.gpsimd.tensor_copy(out=KT[64:96, hd, c0:c0 + T], in_=kper[0:32, :]))

            def sqfn_q(c, s):
                tr.op("act", [P(bq[c])], [("sq", s)],
                      lambda: nc.scalar.activation(out=sq[:, s, :], in_=PS[bq[c]][:], func=AF.Square))
            rms_stats(sqfn_q, 2, onesQ, ("onesQ",))
            for c in range(2):
                tr.op("dve", [P(bq[c]), ("rstd",), ("gains",)], [("cqn", c)],
                      lambda c=c: nc.vector.scalar_tensor_tensor(
                          out=cqn[:, c, :], in0=PS[bq[c]][:], scalar=gains[:, 32 + c:33 + c], in1=rstd[:],
                          op0=ALU.mult, op1=ALU.mult))

            def sqfn_kv(c, s):
                tr.op("act", [P(bkv)], [("sq", s)],
                      lambda: nc.scalar.activation(out=sq[:, s, :], in_=PS[bkv][:], func=AF.Square))
            rms_stats(sqfn_kv, 1, onesK, ("onesK",))
            tr.op("dve", [P(bkv), ("rstd",), ("gains",)], [("ckvn",)],
                  lambda: nc.vector.scalar_tensor_tensor(
                      out=ckvn[:], in0=PS[bkv][:], scalar=gains[:, 34:35], in1=rstd[:],
                      op0=ALU.mult, op1=ALU.mult))
            slot, skey = stA.get()
            wks = slot[:, 0:2048].rearrange("p (k f) -> p k f", k=8)
            bk = next_bank((3, 4, 5))
            tr.op("pe", UK + [skey], [P(bk)], lambda: proj(wks, (0, 128), 128, bk))
            tr.op("act", [P(bk)], [("ks", 0, ti)],
                  lambda: nc.scalar.copy(out=ksT[0:64, 0, c0:c0 + T], in_=PS[bk][0:64, :]))
            tr.op("act", [P(bk)], [("ks", 1, ti)],
                  lambda: nc.scalar.copy(out=ksT[0:64, 1, c0:c0 + T], in_=PS[bk][64:128, :]))
            for j in range(4):
                bv = next_bank((3, 4, 5))
                nblk = 4 * ti + j

                def mm():
                    inst = None
                    for kc in range(8):
                        inst = nc.tensor.matmul(PS[bv][:, 0:128], lhsT=u[:, kc, j * 128:(j + 1) * 128],
                                                rhs=wks[:, kc, 128:256], start=(kc == 0), stop=(kc == 7))
                    return inst
                tr.op("pe", UK + [skey], [P(bv)], mm)
                tr.op("act", [P(bv)], [("VS", nblk)],
                      lambda: nc.scalar.copy(out=VS[:, nblk, :, 0:64],
                                             in_=PS[bv][:, 0:128].rearrange("p (g d) -> p g d", g=2)))
            stA.release()
            slot, skey = stA.get()
            wq = slot[:, 0:2048].rearrange("p (k f) -> p k f", k=2)
            CQK = [("cqn", 0), ("cqn", 1), skey]

            def qmm(bank, lo):
                inst = None
                for kc in range(2):
                    inst = nc.tensor.matmul(PS[bank][:], lhsT=wq[:, kc, lo:lo + 128], rhs=cqn[:, kc, :],
                                            start=(kc == 0), stop=(kc == 1))
                return inst
            for qd in range(2):
                ba = next_bank(PB)
                bb = next_bank(PB)
                tr.op("pe", CQK, [P(ba)], lambda: qmm(ba, 512 + qd * 128))
                tr.op("pe", CQK, [P(bb)], lambda: qmm(bb, 768 + qd * 128))
                tr.op("dve", [P(ba), ("rope",)], [T1K],
                      lambda: nc.vector.tensor_tensor(out=t1[:, :], in0=PS[ba][:], in1=rope[:, 0, :], op=ALU.mult))
                tr.op("dve", [P(bb), ("rope",)], [T2K],
                      lambda: nc.vector.tensor_tensor(out=t2[:, :], in0=PS[bb][:], in1=rope[:, 1, :], op=ALU.mult))
                tr.op("dve", [T1K, T2K], [("sq", qd)],
                      lambda: nc.vector.tensor_tensor(out=sq[:, qd, :], in0=t1[:, :], in1=t2[:, :], op=ALU.add))
                for a4 in range(4):
                    hd = 4 * qd + a4
                    tr.op("pool", [("sq", qd)], [("QTp", hd)],
                          lambda a4=a4, hd=hd: nc.gpsimd.tensor_copy(out=QT[64:96, hd, :],
                                                                     in_=sq[32 * a4:32 * a4 + 32, qd, :]))
            for pr in range(4):
                bn = next_bank(PB)
                tr.op("pe", CQK, [P(bn)], lambda: qmm(bn, pr * 128))
                tr.op("act", [P(bn)], [("QTn", 2 * pr)],
                      lambda: nc.scalar.copy(out=QT[0:64, 2 * pr, :], in_=PS[bn][0:64, :]))
                tr.op("act", [P(bn)], [("QTn", 2 * pr + 1)],
                      lambda: nc.scalar.copy(out=QT[0:64, 2 * pr + 1, :], in_=PS[bn][64:128, :]))
            stA.release()
            slot, skey = stA.get()
            wkv = slot[:, 0:1024]
            for pr in range(4):
                bk = next_bank(PB)
                tr.op("pe", [("ckvn",), skey], [P(bk)],
                      lambda: nc.tensor.matmul(PS[bk][:], lhsT=wkv[:, pr * 128:(pr + 1) * 128], rhs=ckvn[:],
                                               start=True, stop=True))
                tr.op("dve", [P(bk)], [("KTn", 2 * pr, ti)],
                      lambda: nc.vector.tensor_copy(out=KT[0:64, 2 * pr, c0:c0 + T], in_=PS[bk][0:64, :]))
                tr.op("dve", [P(bk)], [("KTn", 2 * pr + 1, ti)],
                      lambda: nc.vector.tensor_copy(out=KT[0:64, 2 * pr + 1, c0:c0 + T], in_=PS[bk][64:128, :]))
            for j in range(4):
                bv = next_bank(PB)
                kb = 4 * ti + j
                tr.op("pe", [("ckvn",), skey], [P(bv)],
                      lambda: nc.tensor.matmul(PS[bv][:], lhsT=ckvn[:, j * 128:(j + 1) * 128], rhs=wkv[:, 512:1024],
                                               start=True, stop=True))
                if j % 2 == 0:
                    tr.op("act", [P(bv)], [("VA", kb)],
                          lambda: nc.scalar.copy(out=VA[:, kb, :, 0:64],
                                                 in_=PS[bv][:].rearrange("p (h d) -> p h d", h=8)))
                else:
                    tr.op("dve", [P(bv)], [("VA", kb)],
                          lambda: nc.vector.tensor_copy(out=VA[:, kb, :, 0:64],
                                                        in_=PS[bv][:].rearrange("p (h d) -> p h d", h=8)))
            stA.release()
            bank_rr[0] = 0

            nkb = 4 * ti + 4
            steps = [(hd, kb) for hd in range(8) for kb in range(nkb)]
            LA = 2
            sc_info = {}
            pt_rr = [0]

            def emit_scores(i):
                hd, kb = steps[i]
                j0 = max(0, kb - 4 * ti)
                q0 = j0 * 128
                N = T - q0
                diag = kb >= 4 * ti
                sbk = next_bank()
                kti = kb // 4

                def mm():
                    inst = nc.tensor.matmul(PS[sbk][:, 0:N], lhsT=KT[0:96, hd, kb * 128:(kb + 1) * 128],
                                            rhs=QT[0:96, hd, q0:T], start=True, stop=not diag)
                    if diag:
                        inst = nc.tensor.matmul(PS[sbk][:, 0:128], lhsT=ident[:], rhs=cmask[:], start=False,
                                                stop=True)
                    return inst
                tr.op("pe", [("KTn", hd, kti), ("KTpe", hd, kti), ("QTn", hd), ("QTp", hd), ("ident",), ("cmask",)],
                      [P(sbk)], mm)
                ps = pt_rr[0] % HCH
                pt_rr[0] += 1
                tr.op("act", [P(sbk)], [("act", ps)],
                      lambda: nc.scalar.activation(out=PT[:, ps, 0:N], in_=PS[sbk][:, 0:N], func=AF.Exp,
                                                   scale=SC_MLA))
                sc_info[i] = (ps, j0)

            def emit_pv(i):
                hd, kb = steps[i]
                ps, j0 = sc_info.pop(i)
                ob = (4, 5)[hd % 2]

                def mm():
                    inst = None
                    for j in range(j0, 4):
                        inst = nc.tensor.matmul(PS[ob][:, j * 65:(j + 1) * 65],
                                                lhsT=PT[:, ps, (j - j0) * 128:(j - j0 + 1) * 128],
                                                rhs=VA[:, kb, hd, :], start=(kb == 0 and j == 0),
                                                stop=(kb == 4 * ti + j), skip_group_check=True)
                    return inst
                tr.op("pe", [("act", ps), ("VA", kb)], [P(ob)], mm)
                if kb == nkb - 1:
                    rs = hd % 2
                    tr.op("dve", [P(ob)], [("rden", rs)],
                          lambda: nc.vector.reciprocal(
                              out=rden[:, rs, :],
                              in_=PS[ob][:, 0:260].rearrange("p (j c) -> p j c", c=65)[:, :, 64]))
                    tr.op("dve", [P(ob), ("rden", rs)], [("om", j, hd) for j in range(4)],
                          lambda: nc.vector.tensor_tensor(
                              out=om[:, :, hd * 64:(hd + 1) * 64],
                              in0=PS[ob][:, 0:260].rearrange("p (j c) -> p j c", c=65)[:, :, 0:64],
                              in1=rden[:, rs, :].unsqueeze(2).to_broadcast([128, 4, 64]), op=ALU.mult))

            for i in range(len(steps) + LA):
                if i < len(steps):
                    emit_scores(i)
                if i >= LA:
                    emit_pv(i - LA)

            slot, skey = stA.get()
            wqs = slot[:].rearrange("p (k f) -> p k f", k=8)
            for pr in range(4):
                bk = next_bank()
                tr.op("pe", UK + [skey], [P(bk)], lambda: proj(wqs, (pr * 128, (pr + 1) * 128), 128, bk))
                tr.op("act", [P(bk)], [("QTn", 2 * pr)],
                      lambda: nc.scalar.copy(out=QT[0:64, 2 * pr, :], in_=PS[bk][0:64, :]))
                tr.op("dve", [P(bk)], [("QTn", 2 * pr + 1)],
                      lambda: nc.vector.tensor_copy(out=QT[0:64, 2 * pr + 1, :], in_=PS[bk][64:128, :]))
            stA.release()

            swa_pts = {}
            es_rr = [0]

            def whichs_of(j):
                return ([0] if 4 * ti + j > 0 else []) + [1]

            def swa_scores(j):
                nblk = 4 * ti + j
                for g in range(2):
                    for wh in whichs_of(j):
                        kbk = nblk - 1 + wh
                        kti = kbk // 4
                        sbk = next_bank()

                        def mm():
                            nc.tensor.matmul(PS[sbk][:], lhsT=ksT[0:64, g, kbk * 128:(kbk + 1) * 128],
                                             rhs=QT[0:64, 4 * g:4 * g + 4, j * 128:(j + 1) * 128],
                                             start=True, stop=False)
                            nc.tensor.matmul(PS[sbk][:], lhsT=ident[:],
                                             rhs=Bhi[:, wh, 4 * g:4 * g + 4, :], start=False, stop=False)
                            return nc.tensor.matmul(PS[sbk][:], lhsT=ident[:],
                                                    rhs=Blo[:, wh, 4 * g:4 * g + 4, :], start=False, stop=True)
                        tr.op("pe", [("ks", g, kti)] + [("QTn", 4 * g + hh) for hh in range(4)] +
                              [("ident",), ("Bhi",), ("Blo",)], [P(sbk)], mm)
                        ps = pt_rr[0] % HCH
                        pt_rr[0] += 1
                        tr.op("act", [P(sbk)], [("act", ps)],
                              lambda: nc.scalar.activation(out=PT[:, ps, :], in_=PS[sbk][:], func=AF.Exp,
                                                           scale=SC_SWA))
                        swa_pts[(j, g, wh)] = (ps, kbk)

            def swa_pv(j):
                whichs = whichs_of(j)
                for g in range(2):
                    ob = (4, 5)[g]

                    def pv():
                        inst = None
                        first = True
                        for hh in range(4):
                            for wi, wh in enumerate(whichs):
                                ps, kbk = swa_pts[(j, g, wh)]
                                inst = nc.tensor.matmul(PS[ob][:, hh * 65:(hh + 1) * 65],
                                                        lhsT=PT[:, ps, hh * 128:(hh + 1) * 128],
                                                        rhs=VS[:, kbk, g, :], start=first,
                                                        stop=(wi == len(whichs) - 1), skip_group_check=True)
                                first = False
                        return inst
                    tr.op("pe", [("act", swa_pts[(j, g, wh)][0]) for wh in whichs] +
                          [("VS", swa_pts[(j, g, wh)][1]) for wh in whichs], [P(ob)], pv)

            def swa_norm(j):
                sl = j % 2
                for g in range(2):
                    ob = (4, 5)[g]
                    tr.op("dve", [P(ob), ("esink",)], [("den", g)],
                          lambda: nc.vector.tensor_tensor(
                              out=den[:, g, :], in0=PS[ob][:, 0:260].rearrange("p (j c) -> p j c", c=65)[:, :, 64],
                              in1=esink[:, 4 * g:4 * g + 4], op=ALU.add))
                    tr.op("dve", [("den", g)], [("rden", g)],
                          lambda: nc.vector.reciprocal(out=rden[:, g, :], in_=den[:, g, :]))
                    tr.op("dve", [P(ob), ("rden", g)], [("os", sl, 4 * g + hh) for hh in range(4)],
                          lambda: nc.vector.tensor_tensor(
                              out=osw[:, sl, g * 256:(g + 1) * 256].rearrange("p (h d) -> p h d", h=4),
                              in0=PS[ob][:, 0:260].rearrange("p (h c) -> p h c", c=65)[:, :, 0:64],
                              in1=rden[:, g, :].unsqueeze(2).to_broadcast([128, 4, 64]), op=ALU.mult))

            def swa_finish(j):
                sl = j % 2
                omk = [("om", j, hd) for hd in range(8)]
                osk = [("os", sl, hd) for hd in range(8)]
                tr.op("act", omk, [("sq", 0), ("ssq", sl, 0)],
                      lambda: nc.scalar.activation(out=sq[:, 0, :], in_=om[:, j, :], func=AF.Square,
                                                   accum_out=ssq[:, sl, 0:1]))
                tr.op("act", osk, [("sq", 1), ("ssq", sl, 1)],
                      lambda: nc.scalar.activation(out=sq[:, 1, :], in_=osw[:, sl, :], func=AF.Square,
                                                   accum_out=ssq[:, sl, 1:2]))
                tr.op("act", [("ssq", sl, 0), ("ssq", sl, 1), ("epsb",)], [("rs2", sl)],
                      lambda: nc.scalar.activation(out=rs2[:, sl, :], in_=ssq[:, sl, :], func=AF.Ln,
                                                   bias=epsb[:], scale=1.0 / 512))
                tr.op("act", [("rs2", sl)], [("rs2", sl)],
                      lambda: nc.scalar.activation(out=rs2[:, sl, :], in_=rs2[:, sl, :], func=AF.Exp, scale=-0.5))
                tr.op("dve", omk + [("rs2", sl), ("gout",)], [("onb", sl, 0)],
                      lambda: nc.vector.scalar_tensor_tensor(
                          out=onb[:, sl, 0:512], in0=om[:, j, :], scalar=rs2[:, sl, 0:1], in1=gout[:, 0:512],
                          op0=ALU.mult, op1=ALU.mult))
                tr.op("dve", osk + [("rs2", sl), ("gout",)], [("onb", sl, 1)],
                      lambda: nc.vector.scalar_tensor_tensor(
                          out=onb[:, sl, 512:1024], in0=osw[:, sl, :], scalar=rs2[:, sl, 1:2],
                          in1=gout[:, 512:1024], op0=ALU.mult, op1=ALU.mult))

            def swa_finish_b(j):
                sl = j % 2

                def trn():
                    inst = None
                    for c in range(8):
                        inst = nc.tensor.transpose(PTR[:, c * 128:(c + 1) * 128],
                                                   onb[:, sl, c * 128:(c + 1) * 128], ident[:])
                    return inst
                tr.op("pe", [("onb", sl, 0), ("onb", sl, 1), ("ident",)], [("PTR",)], trn)
                if j % 2 == 0:
                    tr.op("dve", [("PTR",)], [("uo", j)] + UK,
                          lambda: nc.vector.tensor_copy(out=u[:, :, j * 128:(j + 1) * 128],
                                                        in_=PTR[:].rearrange("p (c t) -> p c t", c=8)))
                else:
                    tr.op("act", [("PTR",)], [("uo", j)] + UK,
                          lambda: nc.scalar.copy(out=u[:, :, j * 128:(j + 1) * 128],
                                                 in_=PTR[:].rearrange("p (c t) -> p c t", c=8)))

            swa_scores(0)
            swa_scores(1)
            swa_pv(0)
            swa_norm(0)
            for j in range(4):
                if j + 2 < 4:
                    swa_scores(j + 2)
                if j + 1 < 4:
                    swa_pv(j + 1)
                    swa_norm(j + 1)
                swa_finish(j)
                if j >= 1:
                    swa_finish_b(j - 1)
            swa_finish_b(3)

            rst = ResidStats(n)
            for half in range(2):
                slot, skey = stA.get()
                wo = slot[:].rearrange("p (k d) -> p k d", k=8)
                for dd in range(4):
                    dc = half * 4 + dd
                    bank = next_bank()

                    def mm():
                        inst = None
                        for kc in range(8):
                            inst = nc.tensor.matmul(PS[bank][:], lhsT=wo[:, kc, dd * 128:(dd + 1) * 128],
                                                    rhs=u[:, kc, :], start=(kc == 0), stop=(kc == 7))
                        return inst
                    tr.op("pe", UK + [("uo", jj) for jj in range(4)] + [skey], [P(bank)], mm)
                    rst.after_pe_group()
                    tr.op("dve", [P(bank), ("h", n % 2, dc)], [("h", n % 2, dc)],
                          lambda: nc.vector.tensor_tensor(out=hcur[:, dc, :], in0=PS[bank][:], in1=hcur[:, dc, :],
                                                          op=ALU.add))
                    rst.chunk_done(dc)
                stA.release()
            rst.finish()

        NT = NSEQ * NTI
        h_stats(0)
        make_u(0, 0)
        for n in range(NT):
            si, ti = divmod(n, NTI)
            hcur = hTs[n % 2]
            if n + 1 < NT:
                load_x(n + 1)
            if n == 0:
                ffn(n, filler=tbl_filler)
                tbl_filler(10 ** 6)
                for e in tr.engs:
                    tr._wait(e, tbl_last[0])
            else:
                ffn(n)
            mixer(n)
            make_u(n, 16)

            ffn(n, next_n=(n + 1 if n + 1 < NT else None))
            for kc in range(8):
                tr.op("dve", [("h", n % 2, kc), ("rstd",), ("gains",)], [("h", n % 2, kc)],
                      lambda kc=kc: nc.vector.scalar_tensor_tensor(
                          out=hcur[:, kc, :], in0=hcur[:, kc, :], scalar=gains[:, 24 + kc:25 + kc], in1=rstd[:],
                          op0=ALU.mult, op1=ALU.mult))
            tr.dma("pool", f"xout{n % 2}", HKn(n), [],
                   outT[si, :, ti * T:(ti + 1) * T].rearrange("(kc p) t -> p kc t", p=128), hcur[:])
        for nm in ("xout0", "xout1"):
            if nm in tr.dma_sems:
                s = tr.dma_sems[nm]
                nc.gpsimd.wait_ge(s[0], s[1])
                nc.sync.wait_ge(s[0], s[1])
    return nc


def _t5_bucket(dist):
    n = np.maximum(dist, 0)
    max_exact = 16
    nf = np.maximum(n, 1).astype(np.float32)
    large = max_exact + (np.log(nf / max_exact) / math.log(128 / max_exact) * (32 - max_exact)).astype(np.int32)
    large = np.minimum(large, 31)
    return np.where(n < max_exact, n, large)


def _constants():
    k = np.arange(128)[:, None]
    q = np.arange(128)[None, :]
    ident = np.eye(128, dtype=np.float32)
    cmask = np.where(k > q, NEG, 0.0).astype(np.float32)
    dist_prev = q - k + 128
    dist_cur = q - k
    valid_prev = (dist_prev >= 0) & (dist_prev < 128)
    valid_cur = (dist_cur >= 0) & (dist_cur < 128)
    mi_prev = np.where(valid_prev, 0.0, NEG).astype(np.float32)
    mi_cur = np.where(valid_cur, 0.0, NEG).astype(np.float32)
    cst = np.concatenate([ident, cmask, mi_prev, mi_cur], axis=1)
    bp = _t5_bucket(dist_prev)
    bc = _t5_bucket(dist_cur)
    mb = np.zeros((128, 32, 2, 128), np.float32)
    for b in range(32):
        mb[:, b, 0, :] = ((bp == b) & valid_prev)
        mb[:, b, 1, :] = ((bc == b) & valid_cur)
    pos = np.arange(S, dtype=np.float32)
    inv_freq = (10000.0 ** (-np.arange(0, 32, 2, dtype=np.float32) / 32)).astype(np.float32)
    ang = pos[None, :] * inv_freq[:, None]
    cos = np.cos(ang).astype(np.float32)
    sin = np.sin(ang).astype(np.float32)
    cos_t = np.concatenate([cos, cos], axis=0)
    sin_t = np.concatenate([-sin, sin], axis=0)
    rope = np.zeros((NTI, 4, 32, 2, T), np.float32)
    for ti in range(NTI):
        rope[ti, :, :, 0, :] = cos_t[None, :, ti * T:(ti + 1) * T]
        rope[ti, :, :, 1, :] = sin_t[None, :, ti * T:(ti + 1) * T]
    return cst, mb.reshape(128, 32 * 256), rope.reshape(NTI, 128, 2 * T)


def _kc_layout(w):
    K, F = w.shape
    return np.ascontiguousarray(w.reshape(K // 128, 128, F).transpose(1, 0, 2))


def prepare_shared(inp):
    f32 = np.float32
    sh = {}
    for f, (gn, un, dn) in enumerate([("w_ffn1_gate", "w_ffn1_up", "w_ffn1_down"),
                                      ("w_ffn2_gate", "w_ffn2_up", "w_ffn2_down")]):
        wg = _kc_layout(np.asarray(inp[gn][0], f32))
        wu = _kc_layout(np.asarray(inp[un][0], f32))
        wgu = np.stack([wg.reshape(128, 8, 11, 256), wu.reshape(128, 8, 11, 256)], axis=0)
        wgu = np.ascontiguousarray(wgu.transpose(3, 1, 0, 2, 4)).reshape(11, 128, ASLOT)
        sh[f"wgu{f + 1}"] = wgu
        wd = np.asarray(inp[dn][0], f32).reshape(NFC, 128, 8, 128)
        wd = wd.transpose(2, 1, 0, 3)
        wdh = np.zeros((2, 8, 128, HCH, 128), f32)
        wdh[0] = wd[:, :, 0:12, :]
        wdh[1, :, :, 0:10, :] = wd[:, :, 12:22, :]
        sh[f"wd{f + 1}"] = np.ascontiguousarray(wdh).reshape(16, 128, BSLOT)
    w_in = np.asarray(inp["w_in"][0], f32)
    cq = w_in[:, 0:256]
    ckv = w_in[:, 256:384]
    kpe = w_in[:, 384:416]
    kpe_sw = np.concatenate([kpe[:, 16:32], kpe[:, 0:16]], axis=1)
    qs = w_in[:, 416:928]
    ks = w_in[:, 928:1056]
    vs = w_in[:, 1056:1184]
    g0 = _kc_layout(np.concatenate([cq, ckv, kpe, kpe_sw], axis=1)).reshape(128, 8 * 448)
    g1 = _kc_layout(qs).reshape(128, 8 * 512)
    g2 = _kc_layout(np.concatenate([ks, vs], axis=1)).reshape(128, 8 * 256)
    sh["win"] = np.ascontiguousarray(np.concatenate([g0, g1, g2], axis=1))
    wqb = np.asarray(inp["w_q_b"][0], f32).reshape(256, 8, 96)
    nope, pe = wqb[:, :, 0:64], wqb[:, :, 64:96]
    pe_sw = np.concatenate([pe[:, :, 16:32], pe[:, :, 0:16]], axis=2)
    wq = np.concatenate([nope.reshape(256, 512), pe.reshape(256, 256), pe_sw.reshape(256, 256)], axis=1)
    sh["wqb"] = _kc_layout(wq).reshape(128, 2 * 8 * 128)
    wkvb = np.asarray(inp["w_kv_b"][0], f32).reshape(128, 8, 128)
    sh["wkvb"] = np.ascontiguousarray(
        np.concatenate([wkvb[:, :, 0:64].reshape(128, 512), wkvb[:, :, 64:128].reshape(128, 512)], axis=1))
    wo = _kc_layout(np.asarray(inp["w_o"][0], f32))
    sh["wo"] = np.ascontiguousarray(wo.reshape(128, 8, 2, 512).transpose(2, 0, 1, 3)).reshape(2, 128, ASLOT)

    def gl(g):
        return np.asarray(g, f32).reshape(-1, 128).T
    sh["gains"] = np.ascontiguousarray(np.concatenate(
        [gl(inp["g_ffn1"][0]), gl(inp["g_mix"][0]), gl(inp["g_ffn2"][0]), gl(inp["g_final"]),
         gl(inp["g_q_a"][0]), gl(inp["g_kv_a"][0])], axis=1))
    sh["gout"] = np.ascontiguousarray(np.concatenate(
        [np.asarray(inp["g_out_mla"][0], f32), np.asarray(inp["g_out_swa"][0], f32)])[None, :])
    sh["sinks"] = np.ascontiguousarray(np.asarray(inp["attn_sinks"][0], f32)[None, :])
    sh["relb"] = np.ascontiguousarray(np.asarray(inp["rel_bias"], f32).reshape(1, 256))
    cst, mb, rope = _constants()
    sh["cst"], sh["mb"], sh["rope"] = cst, mb, rope
    return sh


_PROG_CACHE = {}


def kernel(**inputs):
    x = np.asarray(inputs["x"], np.float32)
    B = x.shape[0]
    ncores = 8
    nseq = B // ncores
    if nseq not in _PROG_CACHE:
        _PROG_CACHE[nseq] = build_program(nseq)
    nc = _PROG_CACHE[nseq]
    sh = prepare_shared(inputs)
    in_maps = []
    for c in range(ncores):
        m = dict(sh)
        m["xT"] = np.ascontiguousarray(x[c * nseq:(c + 1) * nseq].transpose(0, 2, 1))
        in_maps.append(m)
    res = run_bass_kernel_spmd(nc, in_maps, core_ids=list(range(ncores)))
    out = np.empty((B, S, D), np.float32)
    for c in range(ncores):
        out[c * nseq:(c + 1) * nseq] = res.results[c]["outT"].transpose(0, 2, 1)
    return out
```

```python
import contextlib
import math

import numpy as np
import concourse.bass as bass
import concourse.mybir as mybir
from concourse.bass_utils import run_bass_kernel_spmd

F32 = mybir.dt.float32
BF16 = mybir.dt.bfloat16
AF = mybir.ActivationFunctionType
ALU = mybir.AluOpType

D = 1024
S = 2048
DFF = 2816
NFC = DFF // 128
T = 512
NTI = S // T
EPS = 1e-6
NEG = -30000.0
SC_MLA = 96.0 ** -0.5
SC_SWA = 0.125
WIN_COLS = 448 + 512 + 256
ASLOT = 4096
BSLOT = 1536
HCH = 12
NA = 3
NB = 3


class Tracker:
    def __init__(self, nc, es):
        self.nc = nc
        self.engs = {"pe": nc.tensor, "act": nc.scalar, "dve": nc.vector, "pool": nc.gpsimd, "sp": nc.sync}
        self.sems = {}
        self.cnt = {}
        for e in ("pe", "act", "dve", "pool"):
            self.sems[e] = es.enter_context(nc.semaphore("sem_" + e))
            self.cnt[e] = 0
        self.es = es
        self.seen = {e: {} for e in self.engs}
        self.last_w = {}
        self.readers = {}
        self.dma_sems = {}
        self.nwaits = 0

    def dma_sem(self, name):
        if name not in self.dma_sems:
            self.dma_sems[name] = [self.es.enter_context(self.nc.semaphore("dsem_" + name)), 0]
        return self.dma_sems[name]

    def _wait(self, e, tok):
        name, sem, val = tok
        if name == "pe" and e == "pe":
            return
        if self.seen[e].get(name, 0) >= val:
            return
        self.engs[e].wait_ge(sem, val)
        self.nwaits += 1
        self.seen[e][name] = val

    def _deps(self, e, reads, writes):
        toks = {}

        def add(tok):
            if tok is None:
                return
            if toks.get(tok[0], (None, None, -1))[2] < tok[2]:
                toks[tok[0]] = tok

        for k in reads:
            add(self.last_w.get(k))
        for k in writes:
            add(self.last_w.get(k))
            for t in self.readers.get(k, {}).values():
                add(t)
        for tok in toks.values():
            self._wait(e, tok)

    def _commit(self, tok, reads, writes):
        for k in reads:
            self.readers.setdefault(k, {})[tok[0]] = tok
        for k in writes:
            self.last_w[k] = tok
            self.readers[k] = {}

    def op(self, e, reads, writes, fn):
        self._deps(e, reads, writes)
        inst = fn()
        self.cnt[e] += 1
        inst.then_inc(self.sems[e], 1)
        tok = (e, self.sems[e], self.cnt[e])
        self._commit(tok, reads, writes)
        return tok

    def dma(self, q, semname, reads, writes, out, in_, multi=False):
        self._deps(q, reads, writes)
        s = self.dma_sem(semname)
        if not multi and s[1] > 0:
            self._wait(q, ("d_" + semname, s[0], s[1]))
        inst = self.engs[q].dma_start(out=out, in_=in_)
        s[1] += 16
        inst.then_inc(s[0], 16)
        tok = ("d_" + semname, s[0], s[1])
        self._commit(tok, reads, writes)
        return tok

    def barrier(self):
        toks = []
        for e in ("pe", "act", "dve", "pool"):
            if self.cnt[e] > 0:
                toks.append((e, self.sems[e], self.cnt[e]))
        for name, (sem, val) in self.dma_sems.items():
            if val > 0 and not name.startswith("cv"):
                toks.append(("d_" + name, sem, val))
        for e in self.engs:
            for tok in toks:
                if tok[0] == "pe" and e == "pe":
                    continue
                self._wait(e, tok)
        self.last_w = {k: v for k, v in self.last_w.items() if k[0] == "scr"}
        self.readers = {}

    def family(self, semname, keys):
        s = self.dma_sems[semname]
        tok = ("d_" + semname, s[0], s[1])
        for k in keys:
            self.last_w[k] = tok


class Stream:
    def __init__(self, tr, name, slots, items):
        self.tr = tr
        self.name = name
        self.slots = slots
        self.items = items
        self.n = len(slots)
        self.issued = 0
        self.cur = 0

    def _issue(self):
        k = self.issued
        if k >= len(self.items):
            return
        src, n, key = self.items[k]
        s = k % self.n
        self.tr.dma("sp", f"{self.name}{s}", [key], [(self.name, s)], self.slots[s][:, 0:n], src)
        self.issued += 1

    def start(self):
        for _ in range(self.n):
            self._issue()

    def get(self):
        k = self.cur
        assert k < self.issued
        s = k % self.n
        return self.slots[s], (self.name, s)

    def release(self):
        self.cur += 1
        self._issue()


def build_program(NSEQ, debug=False):
    nc = bass.Bass("TRN2", target_bir_lowering=False)

    def din(name, shape, dt=F32):
        return nc.dram_tensor(name, list(shape), dt, kind="ExternalInput").ap()

    def dscr(name, shape, dt=BF16):
        return nc.dram_tensor(name, list(shape), dt, kind="Internal").ap()

    xT = din("xT", [NSEQ, D, S])
    outT = nc.dram_tensor("outT", [NSEQ, D, S], F32, kind="ExternalOutput").ap()
    wgu_in = [din("wgu1", [11, 128, ASLOT]), din("wgu2", [11, 128, ASLOT])]
    wd_in = [din("wd1", [16, 128, BSLOT]), din("wd2", [16, 128, BSLOT])]
    win_in = din("win", [128, 8 * WIN_COLS])
    wqb_in = din("wqb", [128, 2 * 8 * 128])
    wkvb_in = din("wkvb", [128, 1024])
    wo_in = din("wo", [2, 128, ASLOT])
    gains_in = din("gains", [128, 35])
    gout_in = din("gout", [1, 1024])
    sinks_in = din("sinks", [1, 8])
    relb_in = din("relb", [1, 256])
    cst_in = din("cst", [128, 4 * 128])
    mb_in = din("mb", [128, 32 * 256])
    rope_in = din("rope", [NTI, 128, 2 * T])

    wgu_s = [dscr("wgu1s", [11, 128, ASLOT]), dscr("wgu2s", [11, 128, ASLOT])]
    wd_s = [dscr("wd1s", [16, 128, BSLOT]), dscr("wd2s", [16, 128, BSLOT])]
    win_s = dscr("wins", [128, 8 * WIN_COLS])
    wqb_s = dscr("wqbs", [128, 2 * 8 * 128])
    wkvb_s = dscr("wkvbs", [128, 1024])
    wo_s = dscr("wos", [2, 128, ASLOT])

    dbg = {}
    with contextlib.ExitStack() as es:
        es.enter_context(nc.allow_low_precision("bf16 matmul operands, fp32 accumulation"))
        tr = Tracker(nc, es)

        def sb(name, shape, dt):
            return es.enter_context(nc.sbuf_tensor("sb_" + name, list(shape), dt))

        gains = sb("gains", [128, 35], F32)
        gout = sb("gout", [128, 1024], F32)
        esink = sb("esink", [128, 8], F32)
        ident = sb("ident", [128, 128], BF16)
        cmask = sb("cmask", [128, 128], BF16)
        Bhi = sb("Bhi", [128, 2, 8, 128], BF16)
        Blo = sb("Blo", [128, 2, 8, 128], BF16)
        onesD = sb("onesD", [128, 128], BF16)
        onesQ = sb("onesQ", [128, 128], BF16)
        onesK = sb("onesK", [128, 128], BF16)
        epsb = sb("epsb", [128, 1], F32)

        PS = [es.enter_context(nc.psum_tensor(f"ps{i}", [128, 512], F32)) for i in range(7)]
        PTR = es.enter_context(nc.psum_tensor("ptr", [128, 1024], BF16))

        def P(b):
            return ("P", b)

        hTs = [sb("hT0", [128, 8, T], F32), sb("hT1", [128, 8, T], F32)]
        tr.dma("pool", "xin0", [], [("h", 0, kc) for kc in range(8)], hTs[0][:],
               xT[0, :, 0:T].rearrange("(kc p) t -> p kc t", p=128))
        for g in range(11):
            tr.dma("pool", f"cva{g}", [], [("scr", "wgu0", g)], wgu_s[0][g], wgu_in[0][g])
        tr.dma("sp", "cst", [], [("gains",)], gains[:], gains_in, multi=True)
        tr.dma("sp", "cst", [], [("gout",)], gout[:], gout_in.partition_broadcast(128), multi=True)
        tr.dma("sp", "cst", [], [("esink",)], esink[:], sinks_in.partition_broadcast(128), multi=True)
        tr.family("cst", [("gains",), ("gout",), ("esink",)])
        tr.dma("pool", "cstc", [], [("ident",)], ident[:], cst_in[:, 0:128], multi=True)
        tr.dma("pool", "cstc", [], [("cmask",)], cmask[:], cst_in[:, 128:256], multi=True)
        tr.family("cstc", [("ident",), ("cmask",)])
        for c in range(16):
            tr.dma("pool", "cvb", [], [("scr", "wd0", c)], wd_s[0][c], wd_in[0][c], multi=True)
        tr.family("cvb", [("scr", "wd0", c) for c in range(16)])
        tr.dma("pool", "cvm", [], [("scr", "win")], win_s, win_in, multi=True)
        tr.dma("pool", "cvm", [], [("scr", "wqb")], wqb_s, wqb_in, multi=True)
        tr.dma("pool", "cvm", [], [("scr", "wkvb")], wkvb_s, wkvb_in, multi=True)
        for c in range(2):
            tr.dma("pool", "cvm", [], [("scr", "wo", c)], wo_s[c], wo_in[c], multi=True)
        tr.family("cvm", [("scr", "win"), ("scr", "wqb"), ("scr", "wkvb"), ("scr", "wo", 0), ("scr", "wo", 1)])
        for g in range(11):
            tr.dma("pool", "cvc", [], [("scr", "wgu1", g)], wgu_s[1][g], wgu_in[1][g], multi=True)
        tr.family("cvc", [("scr", "wgu1", g) for g in range(11)])
        for c in range(16):
            tr.dma("pool", "cvd", [], [("scr", "wd1", c)], wd_s[1][c], wd_in[1][c], multi=True)
        tr.family("cvd", [("scr", "wd1", c) for c in range(16)])

        tr.op("dve", [], [("onesD",)], lambda: nc.vector.memset(onesD[:], 1.0 / 1024))
        tr.op("dve", [], [("onesQ",)], lambda: nc.vector.memset(onesQ[:], 1.0 / 256))
        tr.op("dve", [], [("onesK",)], lambda: nc.vector.memset(onesK[:], 1.0 / 128))
        tr.op("dve", [], [("epsb",)], lambda: nc.vector.memset(epsb[:], EPS))
        tr.op("act", [("esink",)], [("esink",)],
              lambda: nc.scalar.activation(out=esink[:], in_=esink[:], func=AF.Exp))

        with contextlib.ExitStack() as es2:
            mbt = es2.enter_context(nc.sbuf_tensor("mbt", [128, 32, 2, 128], F32))
            Bf = es2.enter_context(nc.sbuf_tensor("Bf", [128, 2, 8, 128], F32))
            rb = es2.enter_context(nc.sbuf_tensor("rb", [128, 256], F32))
            mi = es2.enter_context(nc.sbuf_tensor("mi", [128, 2, 128], F32))
            tr.dma("sp", "cst2", [], [("mbt",)], mbt[:].rearrange("p b w q -> p (b w q)"), mb_in, multi=True)
            tr.dma("sp", "cst2", [], [("rb",)], rb[:], relb_in.partition_broadcast(128), multi=True)
            tr.dma("sp", "cst2", [], [("mi",)], mi[:].rearrange("p w q -> p (w q)"), cst_in[:, 256:512], multi=True)
            tr.family("cst2", [("mbt",), ("rb",), ("mi",)])
            tr.op("dve", [("rb",)], [("rb",)],
                  lambda: nc.vector.tensor_scalar(out=rb[:], in0=rb[:], scalar1=8.0, scalar2=None, op0=ALU.mult))
            for hd in range(8):
                tr.op("dve", [("mi",)], [("Bf", hd)],
                      lambda hd=hd: nc.vector.tensor_copy(out=Bf[:, :, hd, :], in_=mi[:]))
            for b in range(32):
                for hd in range(8):
                    tr.op("dve", [("mbt",), ("rb",), ("Bf", hd)], [("Bf", hd)],
                          lambda b=b, hd=hd: nc.vector.scalar_tensor_tensor(
                              out=Bf[:, :, hd, :], in0=mbt[:, b, :, :], scalar=rb[:, b * 8 + hd:b * 8 + hd + 1],
                              in1=Bf[:, :, hd, :], op0=ALU.mult, op1=ALU.add))
            allBf = [("Bf", hd) for hd in range(8)]
            tr.op("dve", allBf, [("Bhi",)], lambda: nc.vector.tensor_copy(out=Bhi[:], in_=Bf[:]))
            tr.op("dve", allBf + [("Bhi",)], [("Blo",)],
                  lambda: nc.vector.tensor_tensor(out=Blo[:], in0=Bf[:], in1=Bhi[:], op=ALU.subtract))
            tr.barrier()

        KT = sb("KT", [128, 8, S], BF16)
        VA = sb("VA", [128, 16, 8, 65], BF16)
        ksT = sb("ksT", [128, 2, S], BF16)
        VS = sb("VS", [128, 16, 2, 65], BF16)
        rope = sb("rope", [128, 2, T], F32)
        u = sb("u", [128, 8, T], BF16)
        sq = sb("sq", [128, 2, T], BF16)
        rstd = sb("rstd", [128, T], F32)
        rstdN = sb("rstdN", [128, T], F32)
        act = sb("act", [128, HCH, T], BF16)
        sg = sb("sg", [128, 2, T], F32)
        slotA = [sb(f"slotA{i}", [128, ASLOT], BF16) for i in range(NA)]
        slotB = [sb(f"slotB{i}", [128, BSLOT], BF16) for i in range(NB)]
        cqn = sb("cqn", [128, 2, T], BF16)
        ckvn = sb("ckvn", [128, T], BF16)
        kper = sb("kper", [128, T], BF16)
        QT = sb("QT", [128, 8, T], BF16)
        PT = act
        om = sb("om", [128, 4, 512], F32)
        osw = sb("osw", [128, 2, 512], F32)
        onb = sb("onb", [128, 2, 1024], BF16)
        rden = sb("rden", [128, 2, 4], F32)
        den = sb("den", [128, 2, 4], F32)
        ssq = sb("ssq", [128, 2, 2], F32)
        rs2 = sb("rs2", [128, 2, 2], F32)
        tr.op("dve", [], [("VA", kb) for kb in range(16)], lambda: nc.vector.memset(VA[:], 1.0))
        tr.op("dve", [], [("VS", kb) for kb in range(16)], lambda: nc.vector.memset(VS[:], 1.0))

        itemsA = []
        itemsB = []
        for si in range(NSEQ):
            for ti in range(NTI):
                for g in range(11):
                    itemsA.append((wgu_s[0][g], ASLOT, ("scr", "wgu0", g)))
                itemsA.append((win_s[:, 0:8 * 448], 8 * 448, ("scr", "win")))
                itemsA.append((win_s[:, 8 * 960:8 * 1216], 8 * 256, ("scr", "win")))
                itemsA.append((wqb_s, 2048, ("scr", "wqb")))
                itemsA.append((wkvb_s, 1024, ("scr", "wkvb")))
                itemsA.append((win_s[:, 8 * 448:8 * 960], 8 * 512, ("scr", "win")))
                itemsA.append((wo_s[0], ASLOT, ("scr", "wo", 0)))
                itemsA.append((wo_s[1], ASLOT, ("scr", "wo", 1)))
                for g in range(11):
                    itemsA.append((wgu_s[1][g], ASLOT, ("scr", "wgu1", g)))
                for f in range(2):
                    for hf in range(2):
                        for c in range(8):
                            itemsB.append((wd_s[f][hf * 8 + c], BSLOT, ("scr", f"wd{f}", hf * 8 + c)))
        stA = Stream(tr, "A", slotA, itemsA)
        stB = Stream(tr, "B", slotB, itemsB)

        def HKn(n):
            return [("h", n % 2, kc) for kc in range(8)]

        def load_x(n):
            si, ti = divmod(n, NTI)
            tr.dma("pool", f"xin{n % 2}", [], HKn(n), hTs[n % 2][:],
                   xT[si, :, ti * T:(ti + 1) * T].rearrange("(kc p) t -> p kc t", p=128))

        stA.start()
        stB.start()

        bank_rr = [0]

        def next_bank(pool=(0, 1, 2, 3)):
            b = pool[bank_rr[0] % len(pool)]
            bank_rr[0] += 1
            return b

        UK = [("u", kc) for kc in range(8)]

        def stats_finish(bank, rs, rskey):
            tr.op("act", [P(bank), ("epsb",)], [rskey],
                  lambda: nc.scalar.activation(out=rs[:], in_=PS[bank][:], func=AF.Ln, bias=epsb[:], scale=1.0))
            tr.op("act", [rskey], [rskey],
                  lambda: nc.scalar.activation(out=rs[:], in_=rs[:], func=AF.Exp, scale=-0.5))

        def rms_stats(src_sq_fn, nchunks, ones, oneskey, bank=6, rs=None, rskey=("rstd",)):
            rs = rstd if rs is None else rs
            for c in range(nchunks):
                s = c % 2
                src_sq_fn(c, s)
                tr.op("pe", [("sq", s), oneskey], [P(bank)],
                      lambda c=c, s=s: nc.tensor.matmul(PS[bank][:], lhsT=ones[:], rhs=sq[:, s, :],
                                                        start=(c == 0), stop=(c == nchunks - 1)))
            stats_finish(bank, rs, rskey)

        def h_stats(n, bank=6, rs=None, rskey=("rstd",)):
            hcur = hTs[n % 2]

            def sqfn(c, s):
                tr.op("act", [("h", n % 2, c)], [("sq", s)],
                      lambda: nc.scalar.activation(out=sq[:, s, :], in_=hcur[:, c, :], func=AF.Square))
            rms_stats(sqfn, 8, onesD, ("onesD",), bank, rs, rskey)

        class ResidStats:
            def __init__(self, n, bank=6, rs=None, rskey=("rstd",)):
                self.n = n
                self.bank = bank
                self.rs = rstd if rs is None else rs
                self.rskey = rskey
                self.pending = None

            def _mm(self, c, last):
                s = c % 2
                bank = self.bank
                tr.op("pe", [("sq", s), ("onesD",)], [P(bank)],
                      lambda: nc.tensor.matmul(PS[bank][:], lhsT=onesD[:], rhs=sq[:, s, :], start=(c == 0),
                                               stop=last))

            def chunk_done(self, dc):
                hcur = hTs[self.n % 2]
                s = dc % 2
                tr.op("act", [("h", self.n % 2, dc)], [("sq", s)],
                      lambda: nc.scalar.activation(out=sq[:, s, :], in_=hcur[:, dc, :], func=AF.Square))
                self.pending = dc

            def after_pe_group(self):
                if self.pending is not None and self.pending < 7:
                    self._mm(self.pending, False)
                    self.pending = None

            def finish(self):
                self._mm(7, True)
                stats_finish(self.bank, self.rs, self.rskey)

        def pe_k_ops(bank, M, lhs_fn, rhs_fn, rkeys_fn, nk=8):
            for kc in range(nk):
                tr.op("pe", rkeys_fn(kc), [P(bank)],
                      lambda kc=kc: nc.tensor.matmul(PS[bank][0:M, :], lhsT=lhs_fn(kc), rhs=rhs_fn(kc),
                                                     start=(kc == 0), stop=(kc == nk - 1)))

        def make_u(n, gcol, rs=None, rskey=("rstd",)):
            rs = rstd if rs is None else rs
            hcur = hTs[n % 2]
            for kc in range(8):
                tr.op("dve", [("h", n % 2, kc), rskey, ("gains",)], [("u", kc)],
                      lambda kc=kc: nc.vector.scalar_tensor_tensor(
                          out=u[:, kc, :], in0=hcur[:, kc, :], scalar=gains[:, gcol + kc:gcol + kc + 1],
                          in1=rs[:], op0=ALU.mult, op1=ALU.mult))

        def ffn(n, next_n=None):
            hcur = hTs[n % 2]
            rst = ResidStats(n)
            nst = ResidStats(next_n, 6, rstdN, ("rstdN",)) if next_n is not None else None
            halves = [(0, 12), (12, 10)]
            for hf, (c_lo, nch) in enumerate(halves):
                for g in range(nch // 2):
                    slot, skey = stA.get()
                    w = slot[:].rearrange("p (a k f) -> p a k f", a=2, k=8)
                    for c in range(2):
                        lc = 2 * g + c
                        gb = (0, 1)[lc % 2]
                        ub = (2, 3)[lc % 2]

                        def mm(a, bank):
                            inst = None
                            for kc in range(8):
                                inst = nc.tensor.matmul(PS[bank][:], lhsT=w[:, a, kc, c * 128:(c + 1) * 128],
                                                        rhs=u[:, kc, :], start=(kc == 0), stop=(kc == 7))
                            return inst
                        if hf == 0 and lc == 0:
                            pe_k_ops(gb, 128, lambda kc: w[:, 0, kc, c * 128:(c + 1) * 128], lambda kc: u[:, kc, :],
                                     lambda kc: [("u", kc), skey])
                        else:
                            tr.op("pe", UK + [skey], [P(gb)], lambda: mm(0, gb))
                        tr.op("pe", UK + [skey], [P(ub)], lambda: mm(1, ub))
                        s = lc % 2
                        tr.op("act", [P(gb)], [("sg", s)],
                              lambda: nc.scalar.activation(out=sg[:, s, :], in_=PS[gb][:], func=AF.Silu))
                        tr.op("dve", [("sg", s), P(ub)], [("act", lc)],
                              lambda: nc.vector.tensor_tensor(out=act[:, lc, :], in0=sg[:, s, :], in1=PS[ub][:],
                                                              op=ALU.mult))
                    stA.release()
                if hf == 1 and nst is not None:
                    make_u(next_n, 0, rs=rstdN, rskey=("rstdN",))
                AK = [("act", lc) for lc in range(nch)]
                for dc in range(8):
                    slot, skey = stB.get()
                    w = slot[:, 0:nch * 128].rearrange("p (f d) -> p f d", f=nch)
                    bank = (4, 5)[dc % 2]

                    def mm(lo=0, hi=nch):
                        inst = None
                        for lc in range(lo, hi):
                            inst = nc.tensor.matmul(PS[bank][:], lhsT=w[:, lc, :], rhs=act[:, lc, :],
                                                    start=(lc == 0), stop=(lc == nch - 1))
                        return inst
                    if dc == 0:
                        tr.op("pe", AK[:nch - 2] + [skey], [P(bank)], lambda: mm(0, nch - 2))
                        tr.op("pe", AK[nch - 2:] + [skey], [P(bank)], lambda: mm(nch - 2, nch))
                    else:
                        tr.op("pe", AK + [skey], [P(bank)], mm)
                    stB.release()
                    if hf == 1:
                        rst.after_pe_group()
                    elif nst is not None:
                        nst.after_pe_group()
                    tr.op("dve", [P(bank), ("h", n % 2, dc)], [("h", n % 2, dc)],
                          lambda: nc.vector.scalar_tensor_tensor(out=hcur[:, dc, :], in0=PS[bank][:], scalar=0.5,
                                                                 in1=hcur[:, dc, :], op0=ALU.mult, op1=ALU.add))
                    if hf == 1:
                        rst.chunk_done(dc)
                    elif nst is not None:
                        nst.chunk_done(dc)
                if hf == 0 and nst is not None:
                    nst.finish()
            rst.finish()

        def mixer(n):
            si, ti = divmod(n, NTI)
            hcur = hTs[n % 2]
            c0 = ti * T
            t1 = sg[:, 0, :]
            t2 = sg[:, 1, :]
            T1K, T2K = ("sg", 0), ("sg", 1)
            tr.dma("sp", "rope", [], [("rope",)], rope[:].rearrange("p a t -> p (a t)"), rope_in[ti])
            make_u(n, 8)
            PB = (0, 1, 2, 3, 4, 5)
            slot, skey = stA.get()
            w0 = slot[:, 0:8 * 448].rearrange("p (k f) -> p k f", k=8)

            def proj(w, cols, M, bank):
                inst = None
                for kc in range(8):
                    inst = nc.tensor.matmul(PS[bank][0:M, :], lhsT=w[:, kc, cols[0]:cols[1]], rhs=u[:, kc, :],
                                            start=(kc == 0), stop=(kc == 7))
                return inst
            bq = [0, 1]
            pe_k_ops(bq[0], 128, lambda kc: w0[:, kc, 0:128], lambda kc: u[:, kc, :], lambda kc: [("u", kc), skey])
            tr.op("pe", UK + [skey], [P(bq[1])], lambda: proj(w0, (128, 256), 128, bq[1]))
            bkv = 2
            tr.op("pe", UK + [skey], [P(bkv)], lambda: proj(w0, (256, 384), 128, bkv))
            bka = 3
            tr.op("pe", UK + [skey], [P(bka)], lambda: proj(w0, (384, 416), 32, bka))
            bkb = 4
            tr.op("pe", UK + [skey], [P(bkb)], lambda: proj(w0, (416, 448), 32, bkb))
            stA.release()
            bank_rr[0] = 5
            tr.op("dve", [P(bka), ("rope",)], [T1K],
                  lambda: nc.vector.tensor_tensor(out=t1[0:32, :], in0=PS[bka][0:32, :], in1=rope[0:32, 0, :],
                                                  op=ALU.mult))
            tr.op("dve", [P(bkb), ("rope",)], [T2K],
                  lambda: nc.vector.tensor_tensor(out=t2[0:32, :], in0=PS[bkb][0:32, :], in1=rope[0:32, 1, :],
                                                  op=ALU.mult))
            tr.op("dve", [T1K, T2K], [("kper",)],
                  lambda: nc.vector.tensor_tensor(out=kper[0:32, :], in0=t1[0:32, :], in1=t2[0:32, :], op=ALU.add))
            for hd in range(8):
                tr.op("pool", [("kper",)], [("KTpe", hd, ti)],
                      lambda hd=hd: nc.gpsimd.tensor_copy(out=KT[64:96, hd, c0:c0 + T], in_=kper[0:32, :]))

            def sqfn_q(c, s):
                tr.op("act", [P(bq[c])], [("sq", s)],
                      lambda: nc.scalar.activation(out=sq[:, s, :], in_=PS[bq[c]][:], func=AF.Square))
            rms_stats(sqfn_q, 2, onesQ, ("onesQ",))
            for c in range(2):
                tr.op("dve", [P(bq[c]), ("rstd",), ("gains",)], [("cqn", c)],
                      lambda c=c: nc.vector.scalar_tensor_tensor(
                          out=cqn[:, c, :], in0=PS[bq[c]][:], scalar=gains[:, 32 + c:33 + c], in1=rstd[:],
                          op0=ALU.mult, op1=ALU.mult))

            def sqfn_kv(c, s):
                tr.op("act", [P(bkv)], [("sq", s)],
                      lambda: nc.scalar.activation(out=sq[:, s, :], in_=PS[bkv][:], func=AF.Square))
            rms_stats(sqfn_kv, 1, onesK, ("onesK",))
            tr.op("dve", [P(bkv), ("rstd",), ("gains",)], [("ckvn",)],
                  lambda: nc.vector.scalar_tensor_tensor(
                      out=ckvn[:], in0=PS[bkv][:], scalar=gains[:, 34:35], in1=rstd[:],
                      op0=ALU.mult, op1=ALU.mult))
            slot, skey = stA.get()
            wks = slot[:, 0:2048].rearrange("p (k f) -> p k f", k=8)
            bk = next_bank((3, 4, 5))
            tr.op("pe", UK + [skey], [P(bk)], lambda: proj(wks, (0, 128), 128, bk))
            tr.op("act", [P(bk)], [("ks", 0, ti)],
                  lambda: nc.scalar.copy(out=ksT[0:64, 0, c0:c0 + T], in_=PS[bk][0:64, :]))
            tr.op("act", [P(bk)], [("ks", 1, ti)],
                  lambda: nc.scalar.copy(out=ksT[0:64, 1, c0:c0 + T], in_=PS[bk][64:128, :]))
            for j in range(4):
                bv = next_bank((3, 4, 5))
                nblk = 4 * ti + j

                def mm():
                    inst = None
                    for kc in range(8):
                        inst = nc.tensor.matmul(PS[bv][:, 0:128], lhsT=u[:, kc, j * 128:(j + 1) * 128],
                                                rhs=wks[:, kc, 128:256], start=(kc == 0), stop=(kc == 7))
                    return inst
                tr.op("pe", UK + [skey], [P(bv)], mm)
                tr.op("act", [P(bv)], [("VS", nblk)],
                      lambda: nc.scalar.copy(out=VS[:, nblk, :, 0:64],
                                             in_=PS[bv][:, 0:128].rearrange("p (g d) -> p g d", g=2)))
            stA.release()
            slot, skey = stA.get()
            wq = slot[:, 0:2048].rearrange("p (k f) -> p k f", k=2)
            CQK = [("cqn", 0), ("cqn", 1), skey]

            def qmm(bank, lo):
                inst = None
                for kc in range(2):
                    inst = nc.tensor.matmul(PS[bank][:], lhsT=wq[:, kc, lo:lo + 128], rhs=cqn[:, kc, :],
                                            start=(kc == 0), stop=(kc == 1))
                return inst
            for qd in range(2):
                ba = next_bank(PB)
                bb = next_bank(PB)
                tr.op("pe", CQK, [P(ba)], lambda: qmm(ba, 512 + qd * 128))
                tr.op("pe", CQK, [P(bb)], lambda: qmm(bb, 768 + qd * 128))
                tr.op("dve", [P(ba), ("rope",)], [T1K],
                      lambda: nc.vector.tensor_tensor(out=t1[:, :], in0=PS[ba][:], in1=rope[:, 0, :], op=ALU.mult))
                tr.op("dve", [P(bb), ("rope",)], [T2K],
                      lambda: nc.vector.tensor_tensor(out=t2[:, :], in0=PS[bb][:], in1=rope[:, 1, :], op=ALU.mult))
                tr.op("dve", [T1K, T2K], [("sq", qd)],
                      lambda: nc.vector.tensor_tensor(out=sq[:, qd, :], in0=t1[:, :], in1=t2[:, :], op=ALU.add))
                for a4 in range(4):
                    hd = 4 * qd + a4
                    tr.op("pool", [("sq", qd)], [("QTp", hd)],
                          lambda a4=a4, hd=hd: nc.gpsimd.tensor_copy(out=QT[64:96, hd, :],
                                                                     in_=sq[32 * a4:32 * a4 + 32, qd, :]))
            for pr in range(4):
                bn = next_bank(PB)
                tr.op("pe", CQK, [P(bn)], lambda: qmm(bn, pr * 128))
                tr.op("act", [P(bn)], [("QTn", 2 * pr)],
                      lambda: nc.scalar.copy(out=QT[0:64, 2 * pr, :], in_=PS[bn][0:64, :]))
                tr.op("act", [P(bn)], [("QTn", 2 * pr + 1)],
                      lambda: nc.scalar.copy(out=QT[0:64, 2 * pr + 1, :], in_=PS[bn][64:128, :]))
            stA.release()
            slot, skey = stA.get()
            wkv = slot[:, 0:1024]
            for pr in range(4):
                bk = next_bank(PB)
                tr.op("pe", [("ckvn",), skey], [P(bk)],
                      lambda: nc.tensor.matmul(PS[bk][:], lhsT=wkv[:, pr * 128:(pr + 1) * 128], rhs=ckvn[:],
                                               start=True, stop=True))
                tr.op("dve", [P(bk)], [("KTn", 2 * pr, ti)],
                      lambda: nc.vector.tensor_copy(out=KT[0:64, 2 * pr, c0:c0 + T], in_=PS[bk][0:64, :]))
                tr.op("dve", [P(bk)], [("KTn", 2 * pr + 1, ti)],
                      lambda: nc.vector.tensor_copy(out=KT[0:64, 2 * pr + 1, c0:c0 + T], in_=PS[bk][64:128, :]))
            for j in range(4):
                bv = next_bank(PB)
                kb = 4 * ti + j
                tr.op("pe", [("ckvn",), skey], [P(bv)],
                      lambda: nc.tensor.matmul(PS[bv][:], lhsT=ckvn[:, j * 128:(j + 1) * 128], rhs=wkv[:, 512:1024],
                                               start=True, stop=True))
                if j % 2 == 0:
                    tr.op("act", [P(bv)], [("VA", kb)],
                          lambda: nc.scalar.copy(out=VA[:, kb, :, 0:64],
                                                 in_=PS[bv][:].rearrange("p (h d) -> p h d", h=8)))
                else:
                    tr.op("dve", [P(bv)], [("VA", kb)],
                          lambda: nc.vector.tensor_copy(out=VA[:, kb, :, 0:64],
                                                        in_=PS[bv][:].rearrange("p (h d) -> p h d", h=8)))
            stA.release()
            bank_rr[0] = 0

            nkb = 4 * ti + 4
            steps = [(hd, kb) for hd in range(8) for kb in range(nkb)]
            LA = 2
            sc_info = {}
            pt_rr = [0]

            def emit_scores(i):
                hd, kb = steps[i]
                j0 = max(0, kb - 4 * ti)
                q0 = j0 * 128
                N = T - q0
                diag = kb >= 4 * ti
                sbk = next_bank()
                kti = kb // 4

                def mm():
                    inst = nc.tensor.matmul(PS[sbk][:, 0:N], lhsT=KT[0:96, hd, kb * 128:(kb + 1) * 128],
                                            rhs=QT[0:96, hd, q0:T], start=True, stop=not diag)
                    if diag:
                        inst = nc.tensor.matmul(PS[sbk][:, 0:128], lhsT=ident[:], rhs=cmask[:], start=False,
                                                stop=True)
                    return inst
                tr.op("pe", [("KTn", hd, kti), ("KTpe", hd, kti), ("QTn", hd), ("QTp", hd), ("ident",), ("cmask",)],
                      [P(sbk)], mm)
                ps = pt_rr[0] % HCH
                pt_rr[0] += 1
                tr.op("act", [P(sbk)], [("act", ps)],
                      lambda: nc.scalar.activation(out=PT[:, ps, 0:N], in_=PS[sbk][:, 0:N], func=AF.Exp,
                                                   scale=SC_MLA))
                sc_info[i] = (ps, j0)

            def emit_pv(i):
                hd, kb = steps[i]
                ps, j0 = sc_info.pop(i)
                ob = (4, 5)[hd % 2]

                def mm():
                    inst = None
                    for j in range(j0, 4):
                        inst = nc.tensor.matmul(PS[ob][:, j * 65:(j + 1) * 65],
                                                lhsT=PT[:, ps, (j - j0) * 128:(j - j0 + 1) * 128],
                                                rhs=VA[:, kb, hd, :], start=(kb == 0 and j == 0),
                                                stop=(kb == 4 * ti + j), skip_group_check=True)
                    return inst
                tr.op("pe", [("act", ps), ("VA", kb)], [P(ob)], mm)
                if kb == nkb - 1:
                    rs = hd % 2
                    tr.op("dve", [P(ob)], [("rden", rs)],
                          lambda: nc.vector.reciprocal(
                              out=rden[:, rs, :],
                              in_=PS[ob][:, 0:260].rearrange("p (j c) -> p j c", c=65)[:, :, 64]))
                    tr.op("dve", [P(ob), ("rden", rs)], [("om", j, hd) for j in range(4)],
                          lambda: nc.vector.tensor_tensor(
                              out=om[:, :, hd * 64:(hd + 1) * 64],
                              in0=PS[ob][:, 0:260].rearrange("p (j c) -> p j c", c=65)[:, :, 0:64],
                              in1=rden[:, rs, :].unsqueeze(2).to_broadcast([128, 4, 64]), op=ALU.mult))

            for i in range(len(steps) + LA):
                if i < len(steps):
                    emit_scores(i)
                if i >= LA:
                    emit_pv(i - LA)

            slot, skey = stA.get()
            wqs = slot[:].rearrange("p (k f) -> p k f", k=8)
            for pr in range(4):
                bk = next_bank()
                tr.op("pe", UK + [skey], [P(bk)], lambda: proj(wqs, (pr * 128, (pr + 1) * 128), 128, bk))
                tr.op("act", [P(bk)], [("QTn", 2 * pr)],
                      lambda: nc.scalar.copy(out=QT[0:64, 2 * pr, :], in_=PS[bk][0:64, :]))
                tr.op("dve", [P(bk)], [("QTn", 2 * pr + 1)],
                      lambda: nc.vector.tensor_copy(out=QT[0:64, 2 * pr + 1, :], in_=PS[bk][64:128, :]))
            stA.release()

            swa_pts = {}
            es_rr = [0]

            def whichs_of(j):
                return ([0] if 4 * ti + j > 0 else []) + [1]

            def swa_scores(j):
                nblk = 4 * ti + j
                for g in range(2):
                    for wh in whichs_of(j):
                        kbk = nblk - 1 + wh
                        kti = kbk // 4
                        sbk = next_bank()

                        def mm():
                            nc.tensor.matmul(PS[sbk][:], lhsT=ksT[0:64, g, kbk * 128:(kbk + 1) * 128],
                                             rhs=QT[0:64, 4 * g:4 * g + 4, j * 128:(j + 1) * 128],
                                             start=True, stop=False)
                            nc.tensor.matmul(PS[sbk][:], lhsT=ident[:],
                                             rhs=Bhi[:, wh, 4 * g:4 * g + 4, :], start=False, stop=False)
                            return nc.tensor.matmul(PS[sbk][:], lhsT=ident[:],
                                                    rhs=Blo[:, wh, 4 * g:4 * g + 4, :], start=False, stop=True)
                        tr.op("pe", [("ks", g, kti)] + [("QTn", 4 * g + hh) for hh in range(4)] +
                              [("ident",), ("Bhi",), ("Blo",)], [P(sbk)], mm)
                        ps = pt_rr[0] % HCH
                        pt_rr[0] += 1
                        tr.op("act", [P(sbk)], [("act", ps)],
                              lambda: nc.scalar.activation(out=PT[:, ps, :], in_=PS[sbk][:], func=AF.Exp,
                                                           scale=SC_SWA))
                        swa_pts[(j, g, wh)] = (ps, kbk)

            def swa_pv(j):
                whichs = whichs_of(j)
                for g in range(2):
                    ob = (4, 5)[g]

                    def pv():
                        inst = None
                        first = True
                        for hh in range(4):
                            for wi, wh in enumerate(whichs):
                                ps, kbk = swa_pts[(j, g, wh)]
                                inst = nc.tensor.matmul(PS[ob][:, hh * 65:(hh + 1) * 65],
                                                        lhsT=PT[:, ps, hh * 128:(hh + 1) * 128],
                                                        rhs=VS[:, kbk, g, :], start=first,
                                                        stop=(wi == len(whichs) - 1), skip_group_check=True)
                                first = False
                        return inst
                    tr.op("pe", [("act", swa_pts[(j, g, wh)][0]) for wh in whichs] +
                          [("VS", swa_pts[(j, g, wh)][1]) for wh in whichs], [P(ob)], pv)

            def swa_norm(j):
                sl = j % 2
                for g in range(2):
                    ob = (4, 5)[g]
                    tr.op("dve", [P(ob), ("esink",)], [("den", g)],
                          lambda: nc.vector.tensor_tensor(
                              out=den[:, g, :], in0=PS[ob][:, 0:260].rearrange("p (j c) -> p j c", c=65)[:, :, 64],
                              in1=esink[:, 4 * g:4 * g + 4], op=ALU.add))
                    tr.op("dve", [("den", g)], [("rden", g)],
                          lambda: nc.vector.reciprocal(out=rden[:, g, :], in_=den[:, g, :]))
                    tr.op("dve", [P(ob), ("rden", g)], [("os", sl, 4 * g + hh) for hh in range(4)],
                          lambda: nc.vector.tensor_tensor(
                              out=osw[:, sl, g * 256:(g + 1) * 256].rearrange("p (h d) -> p h d", h=4),
                              in0=PS[ob][:, 0:260].rearrange("p (h c) -> p h c", c=65)[:, :, 0:64],
                              in1=rden[:, g, :].unsqueeze(2).to_broadcast([128, 4, 64]), op=ALU.mult))

            def swa_finish(j):
                sl = j % 2
                omk = [("om", j, hd) for hd in range(8)]
                osk = [("os", sl, hd) for hd in range(8)]
                tr.op("act", omk, [("sq", 0), ("ssq", sl, 0)],
                      lambda: nc.scalar.activation(out=sq[:, 0, :], in_=om[:, j, :], func=AF.Square,
                                                   accum_out=ssq[:, sl, 0:1]))
                tr.op("act", osk, [("sq", 1), ("ssq", sl, 1)],
                      lambda: nc.scalar.activation(out=sq[:, 1, :], in_=osw[:, sl, :], func=AF.Square,
                                                   accum_out=ssq[:, sl, 1:2]))
                tr.op("act", [("ssq", sl, 0), ("ssq", sl, 1), ("epsb",)], [("rs2", sl)],
                      lambda: nc.scalar.activation(out=rs2[:, sl, :], in_=ssq[:, sl, :], func=AF.Ln,
                                                   bias=epsb[:], scale=1.0 / 512))
                tr.op("act", [("rs2", sl)], [("rs2", sl)],
                      lambda: nc.scalar.activation(out=rs2[:, sl, :], in_=rs2[:, sl, :], func=AF.Exp, scale=-0.5))
                tr.op("dve", omk + [("rs2", sl), ("gout",)], [("onb", sl, 0)],
                      lambda: nc.vector.scalar_tensor_tensor(
                          out=onb[:, sl, 0:512], in0=om[:, j, :], scalar=rs2[:, sl, 0:1], in1=gout[:, 0:512],
                          op0=ALU.mult, op1=ALU.mult))
                tr.op("dve", osk + [("rs2", sl), ("gout",)], [("onb", sl, 1)],
                      lambda: nc.vector.scalar_tensor_tensor(
                          out=onb[:, sl, 512:1024], in0=osw[:, sl, :], scalar=rs2[:, sl, 1:2],
                          in1=gout[:, 512:1024], op0=ALU.mult, op1=ALU.mult))

            def swa_finish_b(j):
                sl = j % 2

                def trn():
                    inst = None
                    for c in range(8):
                        inst = nc.tensor.transpose(PTR[:, c * 128:(c + 1) * 128],
                                                   onb[:, sl, c * 128:(c + 1) * 128], ident[:])
                    return inst
                tr.op("pe", [("onb", sl, 0), ("onb", sl, 1), ("ident",)], [("PTR",)], trn)
                cw = [("uo", j)] + (UK if j == 0 else [])
                cr = [("PTR",)] + ([("uo", 0)] if j > 0 else [])
                if j % 2 == 0:
                    tr.op("dve", cr, cw,
                          lambda: nc.vector.tensor_copy(out=u[:, :, j * 128:(j + 1) * 128],
                                                        in_=PTR[:].rearrange("p (c t) -> p c t", c=8)))
                else:
                    tr.op("act", cr, cw,
                          lambda: nc.scalar.copy(out=u[:, :, j * 128:(j + 1) * 128],
                                                 in_=PTR[:].rearrange("p (c t) -> p c t", c=8)))

            swa_scores(0)
            swa_scores(1)
            swa_pv(0)
            swa_norm(0)
            for j in range(4):
                if j + 2 < 4:
                    swa_scores(j + 2)
                if j + 1 < 4:
                    swa_pv(j + 1)
                    swa_norm(j + 1)
                swa_finish(j)
                if j >= 1:
                    swa_finish_b(j - 1)
            swa_finish_b(3)

            rst = ResidStats(n)
            for half in range(2):
                slot, skey = stA.get()
                wo = slot[:].rearrange("p (k d) -> p k d", k=8)
                banks = [next_bank() for _ in range(4)]

                def mmc(dd, lo, hi):
                    inst = None
                    for kc in range(8):
                        inst = nc.tensor.matmul(PS[banks[dd]][:, lo:hi], lhsT=wo[:, kc, dd * 128:(dd + 1) * 128],
                                                rhs=u[:, kc, lo:hi], start=(kc == 0), stop=(kc == 7))
                    return inst
                if half == 0:
                    for dd in range(4):
                        tr.op("pe", UK + [("uo", 0), ("uo", 1), ("uo", 2), skey], [P(banks[dd])],
                              lambda dd=dd: mmc(dd, 0, 384))
                for dd in range(4):
                    dc = half * 4 + dd
                    bank = banks[dd]
                    if half == 0:
                        tr.op("pe", UK + [("uo", 3), skey], [P(bank)], lambda: mmc(dd, 384, 512))
                    else:
                        tr.op("pe", UK + [("uo", jj) for jj in range(4)] + [skey], [P(bank)],
                              lambda: mmc(dd, 0, 512))
                    rst.after_pe_group()
                    tr.op("dve", [P(bank), ("h", n % 2, dc)], [("h", n % 2, dc)],
                          lambda: nc.vector.tensor_tensor(out=hcur[:, dc, :], in0=PS[bank][:], in1=hcur[:, dc, :],
                                                          op=ALU.add))
                    rst.chunk_done(dc)
                stA.release()
            rst.finish()

        NT = NSEQ * NTI
        h_stats(0)
        make_u(0, 0)
        for n in range(NT):
            si, ti = divmod(n, NTI)
            hcur = hTs[n % 2]
            if n + 1 < NT:
                load_x(n + 1)
            ffn(n)
            mixer(n)
            make_u(n, 16)

            ffn(n, next_n=(n + 1 if n + 1 < NT else None))
            for kc in range(8):
                tr.op("dve", [("h", n % 2, kc), ("rstd",), ("gains",)], [("h", n % 2, kc)],
                      lambda kc=kc: nc.vector.scalar_tensor_tensor(
                          out=hcur[:, kc, :], in0=hcur[:, kc, :], scalar=gains[:, 24 + kc:25 + kc], in1=rstd[:],
                          op0=ALU.mult, op1=ALU.mult))
            tr.dma("pool", f"xout{n % 2}", HKn(n), [],
                   outT[si, :, ti * T:(ti + 1) * T].rearrange("(kc p) t -> p kc t", p=128), hcur[:])
        for nm in ("xout0", "xout1"):
            if nm in tr.dma_sems:
                s = tr.dma_sems[nm]
                nc.gpsimd.wait_ge(s[0], s[1])
                nc.sync.wait_ge(s[0], s[1])
    return nc


def _t5_bucket(dist):
    n = np.maximum(dist, 0)
    max_exact = 16
    nf = np.maximum(n, 1).astype(np.float32)
    large = max_exact + (np.log(nf / max_exact) / math.log(128 / max_exact) * (32 - max_exact)).astype(np.int32)
    large = np.minimum(large, 31)
    return np.where(n < max_exact, n, large)


def _constants():
    k = np.arange(128)[:, None]
    q = np.arange(128)[None, :]
    ident = np.eye(128, dtype=np.float32)
    cmask = np.where(k > q, NEG, 0.0).astype(np.float32)
    dist_prev = q - k + 128
    dist_cur = q - k
    valid_prev = (dist_prev >= 0) & (dist_prev < 128)
    valid_cur = (dist_cur >= 0) & (dist_cur < 128)
    mi_prev = np.where(valid_prev, 0.0, NEG).astype(np.float32)
    mi_cur = np.where(valid_cur, 0.0, NEG).astype(np.float32)
    cst = np.concatenate([ident, cmask, mi_prev, mi_cur], axis=1)
    bp = _t5_bucket(dist_prev)
    bc = _t5_bucket(dist_cur)
    mb = np.zeros((128, 32, 2, 128), np.float32)
    for b in range(32):
        mb[:, b, 0, :] = ((bp == b) & valid_prev)
        mb[:, b, 1, :] = ((bc == b) & valid_cur)
    pos = np.arange(S, dtype=np.float32)
    inv_freq = (10000.0 ** (-np.arange(0, 32, 2, dtype=np.float32) / 32)).astype(np.float32)
    ang = pos[None, :] * inv_freq[:, None]
    cos = np.cos(ang).astype(np.float32)
    sin = np.sin(ang).astype(np.float32)
    cos_t = np.concatenate([cos, cos], axis=0)
    sin_t = np.concatenate([-sin, sin], axis=0)
    rope = np.zeros((NTI, 4, 32, 2, T), np.float32)
    for ti in range(NTI):
        rope[ti, :, :, 0, :] = cos_t[None, :, ti * T:(ti + 1) * T]
        rope[ti, :, :, 1, :] = sin_t[None, :, ti * T:(ti + 1) * T]
    return cst, mb.reshape(128, 32 * 256), rope.reshape(NTI, 128, 2 * T)


def _kc_layout(w):
    K, F = w.shape
    return np.ascontiguousarray(w.reshape(K // 128, 128, F).transpose(1, 0, 2))


def prepare_shared(inp):
    f32 = np.float32
    sh = {}
    for f, (gn, un, dn) in enumerate([("w_ffn1_gate", "w_ffn1_up", "w_ffn1_down"),
                                      ("w_ffn2_gate", "w_ffn2_up", "w_ffn2_down")]):
        wg = _kc_layout(np.asarray(inp[gn][0], f32))
        wu = _kc_layout(np.asarray(inp[un][0], f32))
        wgu = np.stack([wg.reshape(128, 8, 11, 256), wu.reshape(128, 8, 11, 256)], axis=0)
        wgu = np.ascontiguousarray(wgu.transpose(3, 1, 0, 2, 4)).reshape(11, 128, ASLOT)
        sh[f"wgu{f + 1}"] = wgu
        wd = np.asarray(inp[dn][0], f32).reshape(NFC, 128, 8, 128)
        wd = wd.transpose(2, 1, 0, 3)
        wdh = np.zeros((2, 8, 128, HCH, 128), f32)
        wdh[0] = wd[:, :, 0:12, :]
        wdh[1, :, :, 0:10, :] = wd[:, :, 12:22, :]
        sh[f"wd{f + 1}"] = np.ascontiguousarray(wdh).reshape(16, 128, BSLOT)
    w_in = np.asarray(inp["w_in"][0], f32)
    cq = w_in[:, 0:256]
    ckv = w_in[:, 256:384]
    kpe = w_in[:, 384:416]
    kpe_sw = np.concatenate([kpe[:, 16:32], kpe[:, 0:16]], axis=1)
    qs = w_in[:, 416:928]
    ks = w_in[:, 928:1056]
    vs = w_in[:, 1056:1184]
    g0 = _kc_layout(np.concatenate([cq, ckv, kpe, kpe_sw], axis=1)).reshape(128, 8 * 448)
    g1 = _kc_layout(qs).reshape(128, 8 * 512)
    g2 = _kc_layout(np.concatenate([ks, vs], axis=1)).reshape(128, 8 * 256)
    sh["win"] = np.ascontiguousarray(np.concatenate([g0, g1, g2], axis=1))
    wqb = np.asarray(inp["w_q_b"][0], f32).reshape(256, 8, 96)
    nope, pe = wqb[:, :, 0:64], wqb[:, :, 64:96]
    pe_sw = np.concatenate([pe[:, :, 16:32], pe[:, :, 0:16]], axis=2)
    wq = np.concatenate([nope.reshape(256, 512), pe.reshape(256, 256), pe_sw.reshape(256, 256)], axis=1)
    sh["wqb"] = _kc_layout(wq).reshape(128, 2 * 8 * 128)
    wkvb = np.asarray(inp["w_kv_b"][0], f32).reshape(128, 8, 128)
    sh["wkvb"] = np.ascontiguousarray(
        np.concatenate([wkvb[:, :, 0:64].reshape(128, 512), wkvb[:, :, 64:128].reshape(128, 512)], axis=1))
    wo = _kc_layout(np.asarray(inp["w_o"][0], f32))
    sh["wo"] = np.ascontiguousarray(wo.reshape(128, 8, 2, 512).transpose(2, 0, 1, 3)).reshape(2, 128, ASLOT)

    def gl(g):
        return np.asarray(g, f32).reshape(-1, 128).T
    sh["gains"] = np.ascontiguousarray(np.concatenate(
        [gl(inp["g_ffn1"][0]), gl(inp["g_mix"][0]), gl(inp["g_ffn2"][0]), gl(inp["g_final"]),
         gl(inp["g_q_a"][0]), gl(inp["g_kv_a"][0])], axis=1))
    sh["gout"] = np.ascontiguousarray(np.concatenate(
        [np.asarray(inp["g_out_mla"][0], f32), np.asarray(inp["g_out_swa"][0], f32)])[None, :])
    sh["sinks"] = np.ascontiguousarray(np.asarray(inp["attn_sinks"][0], f32)[None, :])
    sh["relb"] = np.ascontiguousarray(np.asarray(inp["rel_bias"], f32).reshape(1, 256))
    cst, mb, rope = _constants()
    sh["cst"], sh["mb"], sh["rope"] = cst, mb, rope
    return sh


_PROG_CACHE = {}


def kernel(**inputs):
    x = np.asarray(inputs["x"], np.float32)
    B = x.shape[0]
    ncores = 8
    nseq = B // ncores
    if nseq not in _PROG_CACHE:
        _PROG_CACHE[nseq] = build_program(nseq)
    nc = _PROG_CACHE[nseq]
    sh = prepare_shared(inputs)
    in_maps = []
    for c in range(ncores):
        m = dict(sh)
        m["xT"] = np.ascontiguousarray(x[c * nseq:(c + 1) * nseq].transpose(0, 2, 1))
        in_maps.append(m)
    res = run_bass_kernel_spmd(nc, in_maps, core_ids=list(range(ncores)))
    out = np.empty((B, S, D), np.float32)
    for c in range(ncores):
        out[c * nseq:(c + 1) * nseq] = res.results[c]["outT"].transpose(0, 2, 1)
    return out
```

```python
import contextlib
import math

import numpy as np
import concourse.bass as bass
import concourse.mybir as mybir
from concourse.bass_utils import run_bass_kernel_spmd

F32 = mybir.dt.float32
BF16 = mybir.dt.bfloat16
AF = mybir.ActivationFunctionType
ALU = mybir.AluOpType

D = 1024
S = 2048
DFF = 2816
NFC = DFF // 128
T = 512
NTI = S // T
EPS = 1e-6
NEG = -30000.0
SC_MLA = 96.0 ** -0.5
SC_SWA = 0.125
WIN_COLS = 448 + 512 + 256
ASLOT = 4096
BSLOT = 1536
HCH = 12
NA = 3
NB = 3


class Tracker:
    def __init__(self, nc, es):
        self.nc = nc
        self.engs = {"pe": nc.tensor, "act": nc.scalar, "dve": nc.vector, "pool": nc.gpsimd, "sp": nc.sync}
        self.sems = {}
        self.cnt = {}
        for e in ("pe", "act", "dve", "pool"):
            self.sems[e] = es.enter_context(nc.semaphore("sem_" + e))
            self.cnt[e] = 0
        self.es = es
        self.seen = {e: {} for e in self.engs}
        self.last_w = {}
        self.readers = {}
        self.dma_sems = {}
        self.nwaits = 0

    def dma_sem(self, name):
        if name not in self.dma_sems:
            self.dma_sems[name] = [self.es.enter_context(self.nc.semaphore("dsem_" + name)), 0]
        return self.dma_sems[name]

    def _wait(self, e, tok):
        name, sem, val = tok
        if name == "pe" and e == "pe":
            return
        if self.seen[e].get(name, 0) >= val:
            return
        self.engs[e].wait_ge(sem, val)
        self.nwaits += 1
        self.seen[e][name] = val

    def _deps(self, e, reads, writes):
        toks = {}

        def add(tok):
            if tok is None:
                return
            if toks.get(tok[0], (None, None, -1))[2] < tok[2]:
                toks[tok[0]] = tok

        for k in reads:
            add(self.last_w.get(k))
        for k in writes:
            add(self.last_w.get(k))
            for t in self.readers.get(k, {}).values():
                add(t)
        for tok in toks.values():
            self._wait(e, tok)

    def _commit(self, tok, reads, writes):
        for k in reads:
            self.readers.setdefault(k, {})[tok[0]] = tok
        for k in writes:
            self.last_w[k] = tok
            self.readers[k] = {}

    def op(self, e, reads, writes, fn):
        self._deps(e, reads, writes)
        inst = fn()
        self.cnt[e] += 1
        inst.then_inc(self.sems[e], 1)
        tok = (e, self.sems[e], self.cnt[e])
        self._commit(tok, reads, writes)
        return tok

    def dma(self, q, semname, reads, writes, out, in_, multi=False):
        self._deps(q, reads, writes)
        s = self.dma_sem(semname)
        if not multi and s[1] > 0:
            self._wait(q, ("d_" + semname, s[0], s[1]))
        inst = self.engs[q].dma_start(out=out, in_=in_)
        s[1] += 16
        inst.then_inc(s[0], 16)
        tok = ("d_" + semname, s[0], s[1])
        self._commit(tok, reads, writes)
        return tok

    def barrier(self):
        toks = []
        for e in ("pe", "act", "dve", "pool"):
            if self.cnt[e] > 0:
                toks.append((e, self.sems[e], self.cnt[e]))
        for name, (sem, val) in self.dma_sems.items():
            if val > 0 and not name.startswith("cv"):
                toks.append(("d_" + name, sem, val))
        for e in self.engs:
            for tok in toks:
                if tok[0] == "pe" and e == "pe":
                    continue
                self._wait(e, tok)
        self.last_w = {k: v for k, v in self.last_w.items() if k[0] == "scr"}
        self.readers = {}

    def family(self, semname, keys):
        s = self.dma_sems[semname]
        tok = ("d_" + semname, s[0], s[1])
        for k in keys:
            self.last_w[k] = tok


class Stream:
    def __init__(self, tr, name, slots, items):
        self.tr = tr
        self.name = name
        self.slots = slots
        self.items = items
        self.n = len(slots)
        self.issued = 0
        self.cur = 0

    def _issue(self):
        k = self.issued
        if k >= len(self.items):
            return
        src, n, key = self.items[k]
        s = k % self.n
        self.tr.dma("sp", f"{self.name}{s}", [key], [(self.name, s)], self.slots[s][:, 0:n], src)
        self.issued += 1

    def start(self):
        for _ in range(self.n):
            self._issue()

    def get(self):
        k = self.cur
        assert k < self.issued
        s = k % self.n
        return self.slots[s], (self.name, s)

    def release(self):
        self.cur += 1
        self._issue()


def build_program(NSEQ, debug=False):
    nc = bass.Bass("TRN2", target_bir_lowering=False)

    def din(name, shape, dt=F32):
        return nc.dram_tensor(name, list(shape), dt, kind="ExternalInput").ap()

    def dscr(name, shape, dt=BF16):
        return nc.dram_tensor(name, list(shape), dt, kind="Internal").ap()

    xT = din("xT", [NSEQ, D, S])
    outT = nc.dram_tensor("outT", [NSEQ, D, S], F32, kind="ExternalOutput").ap()
    wgu_in = [din("wgu1", [11, 128, ASLOT]), din("wgu2", [11, 128, ASLOT])]
    wd_in = [din("wd1", [16, 128, BSLOT]), din("wd2", [16, 128, BSLOT])]
    win_in = din("win", [128, 8 * WIN_COLS])
    wqb_in = din("wqb", [128, 2 * 8 * 128])
    wkvb_in = din("wkvb", [128, 1024])
    wo_in = din("wo", [2, 128, ASLOT])
    gains_in = din("gains", [128, 35])
    gout_in = din("gout", [1, 1024])
    sinks_in = din("sinks", [1, 8])
    relb_in = din("relb", [1, 256])
    cst_in = din("cst", [128, 4 * 128])
    mb_in = din("mb", [128, 32 * 256])
    rope_in = din("rope", [NTI, 128, 2 * T])

    wgu_s = [dscr("wgu1s", [11, 128, ASLOT]), dscr("wgu2s", [11, 128, ASLOT])]
    wd_s = [dscr("wd1s", [16, 128, BSLOT]), dscr("wd2s", [16, 128, BSLOT])]
    win_s = dscr("wins", [128, 8 * WIN_COLS])
    wqb_s = dscr("wqbs", [128, 2 * 8 * 128])
    wkvb_s = dscr("wkvbs", [128, 1024])
    wo_s = dscr("wos", [2, 128, ASLOT])

    dbg = {}
    with contextlib.ExitStack() as es:
        es.enter_context(nc.allow_low_precision("bf16 matmul operands, fp32 accumulation"))
        tr = Tracker(nc, es)

        def sb(name, shape, dt):
            return es.enter_context(nc.sbuf_tensor("sb_" + name, list(shape), dt))

        gains = sb("gains", [128, 35], F32)
        gout = sb("gout", [128, 1024], F32)
        esink = sb("esink", [128, 8], F32)
        ident = sb("ident", [128, 128], BF16)
        cmask = sb("cmask", [128, 128], BF16)
        Bhi = sb("Bhi", [128, 2, 8, 128], BF16)
        Blo = sb("Blo", [128, 2, 8, 128], BF16)
        onesD = sb("onesD", [128, 128], BF16)
        onesQ = sb("onesQ", [128, 128], BF16)
        onesK = sb("onesK", [128, 128], BF16)
        epsb = sb("epsb", [128, 1], F32)

        PS = [es.enter_context(nc.psum_tensor(f"ps{i}", [128, 512], F32)) for i in range(7)]
        PTR = es.enter_context(nc.psum_tensor("ptr", [128, 1024], BF16))

        def P(b):
            return ("P", b)

        hTs = [sb("hT0", [128, 8, T], F32), sb("hT1", [128, 8, T], F32)]
        tr.dma("pool", "xin0", [], [("h", 0, kc) for kc in range(8)], hTs[0][:],
               xT[0, :, 0:T].rearrange("(kc p) t -> p kc t", p=128))
        for g in range(11):
            tr.dma("pool", f"cva{g}", [], [("scr", "wgu0", g)], wgu_s[0][g], wgu_in[0][g])
        tr.dma("sp", "cst", [], [("gains",)], gains[:], gains_in, multi=True)
        tr.dma("sp", "cst", [], [("gout",)], gout[:], gout_in.partition_broadcast(128), multi=True)
        tr.dma("sp", "cst", [], [("esink",)], esink[:], sinks_in.partition_broadcast(128), multi=True)
        tr.family("cst", [("gains",), ("gout",), ("esink",)])
        tr.dma("pool", "cstc", [], [("ident",)], ident[:], cst_in[:, 0:128], multi=True)
        tr.dma("pool", "cstc", [], [("cmask",)], cmask[:], cst_in[:, 128:256], multi=True)
        tr.family("cstc", [("ident",), ("cmask",)])
        for c in range(16):
            tr.dma("pool", "cvb", [], [("scr", "wd0", c)], wd_s[0][c], wd_in[0][c], multi=True)
        tr.family("cvb", [("scr", "wd0", c) for c in range(16)])
        tr.dma("pool", "cvm", [], [("scr", "win")], win_s, win_in, multi=True)
        tr.dma("pool", "cvm", [], [("scr", "wqb")], wqb_s, wqb_in, multi=True)
        tr.dma("pool", "cvm", [], [("scr", "wkvb")], wkvb_s, wkvb_in, multi=True)
        for c in range(2):
            tr.dma("pool", "cvm", [], [("scr", "wo", c)], wo_s[c], wo_in[c], multi=True)
        tr.family("cvm", [("scr", "win"), ("scr", "wqb"), ("scr", "wkvb"), ("scr", "wo", 0), ("scr", "wo", 1)])
        for g in range(11):
            tr.dma("pool", "cvc", [], [("scr", "wgu1", g)], wgu_s[1][g], wgu_in[1][g], multi=True)
        tr.family("cvc", [("scr", "wgu1", g) for g in range(11)])
        for c in range(16):
            tr.dma("pool", "cvd", [], [("scr", "wd1", c)], wd_s[1][c], wd_in[1][c], multi=True)
        tr.family("cvd", [("scr", "wd1", c) for c in range(16)])

        tr.op("dve", [], [("onesD",)], lambda: nc.vector.memset(onesD[:], 1.0 / 1024))
        tr.op("dve", [], [("onesQ",)], lambda: nc.vector.memset(onesQ[:], 1.0 / 256))
        tr.op("dve", [], [("onesK",)], lambda: nc.vector.memset(onesK[:], 1.0 / 128))
        tr.op("dve", [], [("epsb",)], lambda: nc.vector.memset(epsb[:], EPS))
        tr.op("act", [("esink",)], [("esink",)],
              lambda: nc.scalar.activation(out=esink[:], in_=esink[:], func=AF.Exp))

        with contextlib.ExitStack() as es2:
            mbt = es2.enter_context(nc.sbuf_tensor("mbt", [128, 32, 2, 128], F32))
            Bf = es2.enter_context(nc.sbuf_tensor("Bf", [128, 2, 8, 128], F32))
            rb = es2.enter_context(nc.sbuf_tensor("rb", [128, 256], F32))
            mi = es2.enter_context(nc.sbuf_tensor("mi", [128, 2, 128], F32))
            tr.dma("sp", "cst2", [], [("mbt",)], mbt[:].rearrange("p b w q -> p (b w q)"), mb_in, multi=True)
            tr.dma("sp", "cst2", [], [("rb",)], rb[:], relb_in.partition_broadcast(128), multi=True)
            tr.dma("sp", "cst2", [], [("mi",)], mi[:].rearrange("p w q -> p (w q)"), cst_in[:, 256:512], multi=True)
            tr.family("cst2", [("mbt",), ("rb",), ("mi",)])
            tr.op("dve", [("rb",)], [("rb",)],
                  lambda: nc.vector.tensor_scalar(out=rb[:], in0=rb[:], scalar1=8.0, scalar2=None, op0=ALU.mult))
            for hd in range(8):
                tr.op("dve", [("mi",)], [("Bf", hd)],
                      lambda hd=hd: nc.vector.tensor_copy(out=Bf[:, :, hd, :], in_=mi[:]))
            for b in range(32):
                for hd in range(8):
                    tr.op("dve", [("mbt",), ("rb",), ("Bf", hd)], [("Bf", hd)],
                          lambda b=b, hd=hd: nc.vector.scalar_tensor_tensor(
                              out=Bf[:, :, hd, :], in0=mbt[:, b, :, :], scalar=rb[:, b * 8 + hd:b * 8 + hd + 1],
                              in1=Bf[:, :, hd, :], op0=ALU.mult, op1=ALU.add))
            allBf = [("Bf", hd) for hd in range(8)]
            tr.op("dve", allBf, [("Bhi",)], lambda: nc.vector.tensor_copy(out=Bhi[:], in_=Bf[:]))
            tr.op("dve", allBf + [("Bhi",)], [("Blo",)],
                  lambda: nc.vector.tensor_tensor(out=Blo[:], in0=Bf[:], in1=Bhi[:], op=ALU.subtract))
            tr.barrier()

        KT = sb("KT", [128, 8, S], BF16)
        VA = sb("VA", [128, 16, 8, 65], BF16)
        ksT = sb("ksT", [128, 2, S], BF16)
        VS = sb("VS", [128, 16, 2, 65], BF16)
        rope = sb("rope", [128, 2, T], F32)
        u = sb("u", [128, 8, T], BF16)
        sq = sb("sq", [128, 2, T], BF16)
        rstd = sb("rstd", [128, T], F32)
        rstdN = sb("rstdN", [128, T], F32)
        act = sb("act", [128, HCH, T], BF16)
        sg = sb("sg", [128, 2, T], F32)
        slotA = [sb(f"slotA{i}", [128, ASLOT], BF16) for i in range(NA)]
        slotB = [sb(f"slotB{i}", [128, BSLOT], BF16) for i in range(NB)]
        cqn = sb("cqn", [128, 2, T], BF16)
        ckvn = sb("ckvn", [128, T], BF16)
        kper = sb("kper", [128, T], BF16)
        QT = sb("QT", [128, 8, T], BF16)
        PT = act
        om = sb("om", [128, 4, 512], F32)
        osw = sb("osw", [128, 2, 512], F32)
        onb = sb("onb", [128, 2, 1024], BF16)
        rden = sb("rden", [128, 2, 4], F32)
        den = sb("den", [128, 2, 4], F32)
        ssq = sb("ssq", [128, 2, 2], F32)
        rs2 = sb("rs2", [128, 2, 2], F32)
        tr.op("dve", [], [("VA", kb) for kb in range(16)], lambda: nc.vector.memset(VA[:], 1.0))
        tr.op("dve", [], [("VS", kb) for kb in range(16)], lambda: nc.vector.memset(VS[:], 1.0))

        itemsA = []
        itemsB = []
        for si in range(NSEQ):
            for ti in range(NTI):
                for g in range(11):
                    itemsA.append((wgu_s[0][g], ASLOT, ("scr", "wgu0", g)))
                itemsA.append((win_s[:, 0:8 * 448], 8 * 448, ("scr", "win")))
                itemsA.append((win_s[:, 8 * 960:8 * 1216], 8 * 256, ("scr", "win")))
                itemsA.append((wqb_s, 2048, ("scr", "wqb")))
                itemsA.append((wkvb_s, 1024, ("scr", "wkvb")))
                itemsA.append((win_s[:, 8 * 448:8 * 960], 8 * 512, ("scr", "win")))
                itemsA.append((wo_s[0], ASLOT, ("scr", "wo", 0)))
                itemsA.append((wo_s[1], ASLOT, ("scr", "wo", 1)))
                for g in range(11):
                    itemsA.append((wgu_s[1][g], ASLOT, ("scr", "wgu1", g)))
                for f in range(2):
                    for hf in range(2):
                        for c in range(8):
                            itemsB.append((wd_s[f][hf * 8 + c], BSLOT, ("scr", f"wd{f}", hf * 8 + c)))
        stA = Stream(tr, "A", slotA, itemsA)
        stB = Stream(tr, "B", slotB, itemsB)

        def HKn(n):
            return [("h", n % 2, kc) for kc in range(8)]

        def load_x(n):
            si, ti = divmod(n, NTI)
            tr.dma("pool", f"xin{n % 2}", [], HKn(n), hTs[n % 2][:],
                   xT[si, :, ti * T:(ti + 1) * T].rearrange("(kc p) t -> p kc t", p=128))

        stA.start()
        stB.start()

        bank_rr = [0]

        def next_bank(pool=(0, 1, 2, 3)):
            b = pool[bank_rr[0] % len(pool)]
            bank_rr[0] += 1
            return b

        UK = [("u", kc) for kc in range(8)]

        def stats_finish(bank, rs, rskey):
            tr.op("act", [P(bank), ("epsb",)], [rskey],
                  lambda: nc.scalar.activation(out=rs[:], in_=PS[bank][:], func=AF.Ln, bias=epsb[:], scale=1.0))
            tr.op("act", [rskey], [rskey],
                  lambda: nc.scalar.activation(out=rs[:], in_=rs[:], func=AF.Exp, scale=-0.5))

        def rms_stats(src_sq_fn, nchunks, ones, oneskey, bank=6, rs=None, rskey=("rstd",)):
            rs = rstd if rs is None else rs
            for c in range(nchunks):
                s = c % 2
                src_sq_fn(c, s)
                tr.op("pe", [("sq", s), oneskey], [P(bank)],
                      lambda c=c, s=s: nc.tensor.matmul(PS[bank][:], lhsT=ones[:], rhs=sq[:, s, :],
                                                        start=(c == 0), stop=(c == nchunks - 1)))
            stats_finish(bank, rs, rskey)

        def h_stats(n, bank=6, rs=None, rskey=("rstd",)):
            hcur = hTs[n % 2]

            def sqfn(c, s):
                tr.op("act", [("h", n % 2, c)], [("sq", s)],
                      lambda: nc.scalar.activation(out=sq[:, s, :], in_=hcur[:, c, :], func=AF.Square))
            rms_stats(sqfn, 8, onesD, ("onesD",), bank, rs, rskey)

        class ResidStats:
            def __init__(self, n, bank=6, rs=None, rskey=("rstd",)):
                self.n = n
                self.bank = bank
                self.rs = rstd if rs is None else rs
                self.rskey = rskey
                self.pending = None

            def _mm(self, c, last):
                s = c % 2
                bank = self.bank
                tr.op("pe", [("sq", s), ("onesD",)], [P(bank)],
                      lambda: nc.tensor.matmul(PS[bank][:], lhsT=onesD[:], rhs=sq[:, s, :], start=(c == 0),
                                               stop=last))

            def chunk_done(self, dc):
                hcur = hTs[self.n % 2]
                s = dc % 2
                tr.op("act", [("h", self.n % 2, dc)], [("sq", s)],
                      lambda: nc.scalar.activation(out=sq[:, s, :], in_=hcur[:, dc, :], func=AF.Square))
                self.pending = dc

            def after_pe_group(self):
                if self.pending is not None and self.pending < 7:
                    self._mm(self.pending, False)
                    self.pending = None

            def finish(self):
                self._mm(7, True)
                stats_finish(self.bank, self.rs, self.rskey)

        def pe_k_ops(bank, M, lhs_fn, rhs_fn, rkeys_fn, nk=8):
            for kc in range(nk):
                tr.op("pe", rkeys_fn(kc), [P(bank)],
                      lambda kc=kc: nc.tensor.matmul(PS[bank][0:M, :], lhsT=lhs_fn(kc), rhs=rhs_fn(kc),
                                                     start=(kc == 0), stop=(kc == nk - 1)))

        def make_u(n, gcol, rs=None, rskey=("rstd",)):
            rs = rstd if rs is None else rs
            hcur = hTs[n % 2]
            for kc in range(8):
                tr.op("dve", [("h", n % 2, kc), rskey, ("gains",)], [("u", kc)],
                      lambda kc=kc: nc.vector.scalar_tensor_tensor(
                          out=u[:, kc, :], in0=hcur[:, kc, :], scalar=gains[:, gcol + kc:gcol + kc + 1],
                          in1=rs[:], op0=ALU.mult, op1=ALU.mult))

        def ffn(n, next_n=None):
            hcur = hTs[n % 2]
            rst = ResidStats(n)
            nst = ResidStats(next_n, 6, rstdN, ("rstdN",)) if next_n is not None else None
            halves = [(0, 12), (12, 10)]
            for hf, (c_lo, nch) in enumerate(halves):
                for g in range(nch // 2):
                    slot, skey = stA.get()
                    w = slot[:].rearrange("p (a k f) -> p a k f", a=2, k=8)
                    for c in range(2):
                        lc = 2 * g + c
                        gb = (0, 1)[lc % 2]
                        ub = (2, 3)[lc % 2]

                        def mm(a, bank):
                            inst = None
                            for kc in range(8):
                                inst = nc.tensor.matmul(PS[bank][:], lhsT=w[:, a, kc, c * 128:(c + 1) * 128],
                                                        rhs=u[:, kc, :], start=(kc == 0), stop=(kc == 7))
                            return inst
                        if hf == 0 and lc == 0:
                            pe_k_ops(gb, 128, lambda kc: w[:, 0, kc, c * 128:(c + 1) * 128], lambda kc: u[:, kc, :],
                                     lambda kc: [("u", kc), skey])
                        else:
                            tr.op("pe", UK + [skey], [P(gb)], lambda: mm(0, gb))
                        tr.op("pe", UK + [skey], [P(ub)], lambda: mm(1, ub))
                        s = lc % 2
                        tr.op("act", [P(gb)], [("sg", s)],
                              lambda: nc.scalar.activation(out=sg[:, s, :], in_=PS[gb][:], func=AF.Silu))
                        tr.op("dve", [("sg", s), P(ub)], [("act", lc)],
                              lambda: nc.vector.tensor_tensor(out=act[:, lc, :], in0=sg[:, s, :], in1=PS[ub][:],
                                                              op=ALU.mult))
                    stA.release()
                if hf == 1 and nst is not None:
                    make_u(next_n, 0, rs=rstdN, rskey=("rstdN",))
                AK = [("act", lc) for lc in range(nch)]
                for dc in range(8):
                    slot, skey = stB.get()
                    w = slot[:, 0:nch * 128].rearrange("p (f d) -> p f d", f=nch)
                    bank = (4, 5)[dc % 2]

                    def mm(lo=0, hi=nch):
                        inst = None
                        for lc in range(lo, hi):
                            inst = nc.tensor.matmul(PS[bank][:], lhsT=w[:, lc, :], rhs=act[:, lc, :],
                                                    start=(lc == 0), stop=(lc == nch - 1))
                        return inst
                    if dc == 0:
                        tr.op("pe", AK[:nch - 2] + [skey], [P(bank)], lambda: mm(0, nch - 2))
                        tr.op("pe", AK[nch - 2:] + [skey], [P(bank)], lambda: mm(nch - 2, nch))
                    else:
                        tr.op("pe", AK + [skey], [P(bank)], mm)
                    stB.release()
                    if hf == 1:
                        rst.after_pe_group()
                    elif nst is not None:
                        nst.after_pe_group()
                    tr.op("dve", [P(bank), ("h", n % 2, dc)], [("h", n % 2, dc)],
                          lambda: nc.vector.scalar_tensor_tensor(out=hcur[:, dc, :], in0=PS[bank][:], scalar=0.5,
                                                                 in1=hcur[:, dc, :], op0=ALU.mult, op1=ALU.add))
                    if hf == 1:
                        rst.chunk_done(dc)
                    elif nst is not None:
                        nst.chunk_done(dc)
                if hf == 0 and nst is not None:
                    nst.finish()
            rst.finish()

        def mixer(n):
            si, ti = divmod(n, NTI)
            hcur = hTs[n % 2]
            c0 = ti * T
            t1 = sg[:, 0, :]
            t2 = sg[:, 1, :]
            T1K, T2K = ("sg", 0), ("sg", 1)
            tr.dma("sp", "rope", [], [("rope",)], rope[:].rearrange("p a t -> p (a t)"), rope_in[ti])
            make_u(n, 8)
            PB = (0, 1, 2, 3, 4, 5)
            slot, skey = stA.get()
            w0 = slot[:, 0:8 * 448].rearrange("p (k f) -> p k f", k=8)

            def proj(w, cols, M, bank):
                inst = None
                for kc in range(8):
                    inst = nc.tensor.matmul(PS[bank][0:M, :], lhsT=w[:, kc, cols[0]:cols[1]], rhs=u[:, kc, :],
                                            start=(kc == 0), stop=(kc == 7))
                return inst
            bq = [0, 1]
            pe_k_ops(bq[0], 128, lambda kc: w0[:, kc, 0:128], lambda kc: u[:, kc, :], lambda kc: [("u", kc), skey])
            tr.op("pe", UK + [skey], [P(bq[1])], lambda: proj(w0, (128, 256), 128, bq[1]))
            bkv = 2
            tr.op("pe", UK + [skey], [P(bkv)], lambda: proj(w0, (256, 384), 128, bkv))
            bka = 3
            tr.op("pe", UK + [skey], [P(bka)], lambda: proj(w0, (384, 416), 32, bka))
            bkb = 4
            tr.op("pe", UK + [skey], [P(bkb)], lambda: proj(w0, (416, 448), 32, bkb))
            stA.release()
            bank_rr[0] = 5
            tr.op("dve", [P(bka), ("rope",)], [T1K],
                  lambda: nc.vector.tensor_tensor(out=t1[0:32, :], in0=PS[bka][0:32, :], in1=rope[0:32, 0, :],
                                                  op=ALU.mult))
            tr.op("dve", [P(bkb), ("rope",)], [T2K],
                  lambda: nc.vector.tensor_tensor(out=t2[0:32, :], in0=PS[bkb][0:32, :], in1=rope[0:32, 1, :],
                                                  op=ALU.mult))
            tr.op("dve", [T1K, T2K], [("kper",)],
                  lambda: nc.vector.tensor_tensor(out=kper[0:32, :], in0=t1[0:32, :], in1=t2[0:32, :], op=ALU.add))
            for hd in range(8):
                tr.op("pool", [("kper",)], [("KTpe", hd, ti)],
                      lambda hd=hd: nc.gpsimd.tensor_copy(out=KT[64:96, hd, c0:c0 + T], in_=kper[0:32, :]))

            def sqfn_q(c, s):
                tr.op("act", [P(bq[c])], [("sq", s)],
                      lambda: nc.scalar.activation(out=sq[:, s, :], in_=PS[bq[c]][:], func=AF.Square))
            rms_stats(sqfn_q, 2, onesQ, ("onesQ",))
            for c in range(2):
                tr.op("dve", [P(bq[c]), ("rstd",), ("gains",)], [("cqn", c)],
                      lambda c=c: nc.vector.scalar_tensor_tensor(
                          out=cqn[:, c, :], in0=PS[bq[c]][:], scalar=gains[:, 32 + c:33 + c], in1=rstd[:],
                          op0=ALU.mult, op1=ALU.mult))

            def sqfn_kv(c, s):
                tr.op("act", [P(bkv)], [("sq", s)],
                      lambda: nc.scalar.activation(out=sq[:, s, :], in_=PS[bkv][:], func=AF.Square))
            rms_stats(sqfn_kv, 1, onesK, ("onesK",))
            tr.op("dve", [P(bkv), ("rstd",), ("gains",)], [("ckvn",)],
                  lambda: nc.vector.scalar_tensor_tensor(
                      out=ckvn[:], in0=PS[bkv][:], scalar=gains[:, 34:35], in1=rstd[:],
                      op0=ALU.mult, op1=ALU.mult))
            slot, skey = stA.get()
            wks = slot[:, 0:2048].rearrange("p (k f) -> p k f", k=8)
            bk = next_bank((3, 4, 5))
            tr.op("pe", UK + [skey], [P(bk)], lambda: proj(wks, (0, 128), 128, bk))
            tr.op("act", [P(bk)], [("ks", 0, ti)],
                  lambda: nc.scalar.copy(out=ksT[0:64, 0, c0:c0 + T], in_=PS[bk][0:64, :]))
            tr.op("act", [P(bk)], [("ks", 1, ti)],
                  lambda: nc.scalar.copy(out=ksT[0:64, 1, c0:c0 + T], in_=PS[bk][64:128, :]))
            for j in range(4):
                bv = next_bank((3, 4, 5))
                nblk = 4 * ti + j

                def mm():
                    inst = None
                    for kc in range(8):
                        inst = nc.tensor.matmul(PS[bv][:, 0:128], lhsT=u[:, kc, j * 128:(j + 1) * 128],
                                                rhs=wks[:, kc, 128:256], start=(kc == 0), stop=(kc == 7))
                    return inst
                tr.op("pe", UK + [skey], [P(bv)], mm)
                tr.op("act", [P(bv)], [("VS", nblk)],
                      lambda: nc.scalar.copy(out=VS[:, nblk, :, 0:64],
                                             in_=PS[bv][:, 0:128].rearrange("p (g d) -> p g d", g=2)))
            stA.release()
            slot, skey = stA.get()
            wq = slot[:, 0:2048].rearrange("p (k f) -> p k f", k=2)
            CQK = [("cqn", 0), ("cqn", 1), skey]

            def qmm(bank, lo):
                inst = None
                for kc in range(2):
                    inst = nc.tensor.matmul(PS[bank][:], lhsT=wq[:, kc, lo:lo + 128], rhs=cqn[:, kc, :],
                                            start=(kc == 0), stop=(kc == 1))
                return inst
            for qd in range(2):
                ba = next_bank(PB)
                bb = next_bank(PB)
                tr.op("pe", CQK, [P(ba)], lambda: qmm(ba, 512 + qd * 128))
                tr.op("pe", CQK, [P(bb)], lambda: qmm(bb, 768 + qd * 128))
                tr.op("dve", [P(ba), ("rope",)], [T1K],
                      lambda: nc.vector.tensor_tensor(out=t1[:, :], in0=PS[ba][:], in1=rope[:, 0, :], op=ALU.mult))
                tr.op("dve", [P(bb), ("rope",)], [T2K],
                      lambda: nc.vector.tensor_tensor(out=t2[:, :], in0=PS[bb][:], in1=rope[:, 1, :], op=ALU.mult))
                tr.op("dve", [T1K, T2K], [("sq", qd)],
                      lambda: nc.vector.tensor_tensor(out=sq[:, qd, :], in0=t1[:, :], in1=t2[:, :], op=ALU.add))
                for a4 in range(4):
                    hd = 4 * qd + a4
                    tr.op("pool", [("sq", qd)], [("QTp", hd)],
                          lambda a4=a4, hd=hd: nc.gpsimd.tensor_copy(out=QT[64:96, hd, :],
                                                                     in_=sq[32 * a4:32 * a4 + 32, qd, :]))
            for pr in range(4):
                bn = next_bank(PB)
                tr.op("pe", CQK, [P(bn)], lambda: qmm(bn, pr * 128))
                tr.op("act", [P(bn)], [("QTn", 2 * pr)],
                      lambda: nc.scalar.copy(out=QT[0:64, 2 * pr, :], in_=PS[bn][0:64, :]))
                tr.op("act", [P(bn)], [("QTn", 2 * pr + 1)],
                      lambda: nc.scalar.copy(out=QT[0:64, 2 * pr + 1, :], in_=PS[bn][64:128, :]))
            stA.release()
            slot, skey = stA.get()
            wkv = slot[:, 0:1024]
            for pr in range(4):
                bk = next_bank(PB)
                tr.op("pe", [("ckvn",), skey], [P(bk)],
                      lambda: nc.tensor.matmul(PS[bk][:], lhsT=wkv[:, pr * 128:(pr + 1) * 128], rhs=ckvn[:],
                                               start=True, stop=True))
                tr.op("dve", [P(bk)], [("KTn", 2 * pr, ti)],
                      lambda: nc.vector.tensor_copy(out=KT[0:64, 2 * pr, c0:c0 + T], in_=PS[bk][0:64, :]))
                tr.op("dve", [P(bk)], [("KTn", 2 * pr + 1, ti)],
                      lambda: nc.vector.tensor_copy(out=KT[0:64, 2 * pr + 1, c0:c0 + T], in_=PS[bk][64:128, :]))
            for j in range(4):
                bv = next_bank(PB)
                kb = 4 * ti + j
                tr.op("pe", [("ckvn",), skey], [P(bv)],
                      lambda: nc.tensor.matmul(PS[bv][:], lhsT=ckvn[:, j * 128:(j + 1) * 128], rhs=wkv[:, 512:1024],
                                               start=True, stop=True))
                if j % 2 == 0:
                    tr.op("act", [P(bv)], [("VA", kb)],
                          lambda: nc.scalar.copy(out=VA[:, kb, :, 0:64],
                                                 in_=PS[bv][:].rearrange("p (h d) -> p h d", h=8)))
                else:
                    tr.op("dve", [P(bv)], [("VA", kb)],
                          lambda: nc.vector.tensor_copy(out=VA[:, kb, :, 0:64],
                                                        in_=PS[bv][:].rearrange("p (h d) -> p h d", h=8)))
            stA.release()
            bank_rr[0] = 0

            nkb = 4 * ti + 4
            steps = [(hd, kb) for hd in range(8) for kb in range(nkb)]
            LA = 3
            sc_info = {}
            pt_rr = [0]

            def emit_scores(i):
                hd, kb = steps[i]
                j0 = max(0, kb - 4 * ti)
                q0 = j0 * 128
                N = T - q0
                diag = kb >= 4 * ti
                sbk = next_bank()
                kti = kb // 4

                def mm():
                    inst = nc.tensor.matmul(PS[sbk][:, 0:N], lhsT=KT[0:96, hd, kb * 128:(kb + 1) * 128],
                                            rhs=QT[0:96, hd, q0:T], start=True, stop=not diag)
                    if diag:
                        inst = nc.tensor.matmul(PS[sbk][:, 0:128], lhsT=ident[:], rhs=cmask[:], start=False,
                                                stop=True)
                    return inst
                tr.op("pe", [("KTn", hd, kti), ("KTpe", hd, kti), ("QTn", hd), ("QTp", hd), ("ident",), ("cmask",)],
                      [P(sbk)], mm)
                ps = pt_rr[0] % HCH
                pt_rr[0] += 1
                tr.op("act", [P(sbk)], [("act", ps)],
                      lambda: nc.scalar.activation(out=PT[:, ps, 0:N], in_=PS[sbk][:, 0:N], func=AF.Exp,
                                                   scale=SC_MLA))
                sc_info[i] = (ps, j0)

            def emit_pv(i):
                hd, kb = steps[i]
                ps, j0 = sc_info.pop(i)
                ob = (4, 5)[hd % 2]

                def mm():
                    inst = None
                    for j in range(j0, 4):
                        inst = nc.tensor.matmul(PS[ob][:, j * 65:(j + 1) * 65],
                                                lhsT=PT[:, ps, (j - j0) * 128:(j - j0 + 1) * 128],
                                                rhs=VA[:, kb, hd, :], start=(kb == 0 and j == 0),
                                                stop=(kb == 4 * ti + j), skip_group_check=True)
                    return inst
                tr.op("pe", [("act", ps), ("VA", kb)], [P(ob)], mm)
                if kb == nkb - 1:
                    rs = hd % 2
                    tr.op("dve", [P(ob)], [("rden", rs)],
                          lambda: nc.vector.reciprocal(
                              out=rden[:, rs, :],
                              in_=PS[ob][:, 0:260].rearrange("p (j c) -> p j c", c=65)[:, :, 64]))
                    tr.op("dve", [P(ob), ("rden", rs)], [("om", j, hd) for j in range(4)],
                          lambda: nc.vector.tensor_tensor(
                              out=om[:, :, hd * 64:(hd + 1) * 64],
                              in0=PS[ob][:, 0:260].rearrange("p (j c) -> p j c", c=65)[:, :, 0:64],
                              in1=rden[:, rs, :].unsqueeze(2).to_broadcast([128, 4, 64]), op=ALU.mult))

            for i in range(len(steps) + LA):
                if i < len(steps):
                    emit_scores(i)
                if i >= LA:
                    emit_pv(i - LA)

            slot, skey = stA.get()
            wqs = slot[:].rearrange("p (k f) -> p k f", k=8)
            for pr in range(4):
                bk = next_bank()
                tr.op("pe", UK + [skey], [P(bk)], lambda: proj(wqs, (pr * 128, (pr + 1) * 128), 128, bk))
                tr.op("act", [P(bk)], [("QTn", 2 * pr)],
                      lambda: nc.scalar.copy(out=QT[0:64, 2 * pr, :], in_=PS[bk][0:64, :]))
                tr.op("dve", [P(bk)], [("QTn", 2 * pr + 1)],
                      lambda: nc.vector.tensor_copy(out=QT[0:64, 2 * pr + 1, :], in_=PS[bk][64:128, :]))
            stA.release()

            swa_pts = {}
            es_rr = [0]

            def whichs_of(j):
                return ([0] if 4 * ti + j > 0 else []) + [1]

            def swa_scores(j):
                nblk = 4 * ti + j
                for g in range(2):
                    for wh in whichs_of(j):
                        kbk = nblk - 1 + wh
                        kti = kbk // 4
                        sbk = next_bank()

                        def mm():
                            nc.tensor.matmul(PS[sbk][:], lhsT=ksT[0:64, g, kbk * 128:(kbk + 1) * 128],
                                             rhs=QT[0:64, 4 * g:4 * g + 4, j * 128:(j + 1) * 128],
                                             start=True, stop=False)
                            nc.tensor.matmul(PS[sbk][:], lhsT=ident[:],
                                             rhs=Bhi[:, wh, 4 * g:4 * g + 4, :], start=False, stop=False)
                            return nc.tensor.matmul(PS[sbk][:], lhsT=ident[:],
                                                    rhs=Blo[:, wh, 4 * g:4 * g + 4, :], start=False, stop=True)
                        tr.op("pe", [("ks", g, kti)] + [("QTn", 4 * g + hh) for hh in range(4)] +
                              [("ident",), ("Bhi",), ("Blo",)], [P(sbk)], mm)
                        ps = pt_rr[0] % HCH
                        pt_rr[0] += 1
                        tr.op("act", [P(sbk)], [("act", ps)],
                              lambda: nc.scalar.activation(out=PT[:, ps, :], in_=PS[sbk][:], func=AF.Exp,
                                                           scale=SC_SWA))
                        swa_pts[(j, g, wh)] = (ps, kbk)

            def swa_pv(j):
                whichs = whichs_of(j)
                for g in range(2):
                    ob = (4, 5)[g]

                    def pv():
                        inst = None
                        first = True
                        for hh in range(4):
                            for wi, wh in enumerate(whichs):
                                ps, kbk = swa_pts[(j, g, wh)]
                                inst = nc.tensor.matmul(PS[ob][:, hh * 65:(hh + 1) * 65],
                                                        lhsT=PT[:, ps, hh * 128:(hh + 1) * 128],
                                                        rhs=VS[:, kbk, g, :], start=first,
                                                        stop=(wi == len(whichs) - 1), skip_group_check=True)
                                first = False
                        return inst
                    tr.op("pe", [("act", swa_pts[(j, g, wh)][0]) for wh in whichs] +
                          [("VS", swa_pts[(j, g, wh)][1]) for wh in whichs], [P(ob)], pv)

            def swa_norm(j):
                sl = j % 2
                for g in range(2):
                    ob = (4, 5)[g]
                    tr.op("dve", [P(ob), ("esink",)], [("den", g)],
                          lambda: nc.vector.tensor_tensor(
                              out=den[:, g, :], in0=PS[ob][:, 0:260].rearrange("p (j c) -> p j c", c=65)[:, :, 64],
                              in1=esink[:, 4 * g:4 * g + 4], op=ALU.add))
                    tr.op("dve", [("den", g)], [("rden", g)],
                          lambda: nc.vector.reciprocal(out=rden[:, g, :], in_=den[:, g, :]))
                    tr.op("dve", [P(ob), ("rden", g)], [("os", sl, 4 * g + hh) for hh in range(4)],
                          lambda: nc.vector.tensor_tensor(
                              out=osw[:, sl, g * 256:(g + 1) * 256].rearrange("p (h d) -> p h d", h=4),
                              in0=PS[ob][:, 0:260].rearrange("p (h c) -> p h c", c=65)[:, :, 0:64],
                              in1=rden[:, g, :].unsqueeze(2).to_broadcast([128, 4, 64]), op=ALU.mult))

            def swa_finish(j):
                sl = j % 2
                omk = [("om", j, hd) for hd in range(8)]
                osk = [("os", sl, hd) for hd in range(8)]
                tr.op("act", omk, [("sq", 0), ("ssq", sl, 0)],
                      lambda: nc.scalar.activation(out=sq[:, 0, :], in_=om[:, j, :], func=AF.Square,
                                                   accum_out=ssq[:, sl, 0:1]))
                tr.op("act", osk, [("sq", 1), ("ssq", sl, 1)],
                      lambda: nc.scalar.activation(out=sq[:, 1, :], in_=osw[:, sl, :], func=AF.Square,
                                                   accum_out=ssq[:, sl, 1:2]))
                tr.op("act", [("ssq", sl, 0), ("ssq", sl, 1), ("epsb",)], [("rs2", sl)],
                      lambda: nc.scalar.activation(out=rs2[:, sl, :], in_=ssq[:, sl, :], func=AF.Ln,
                                                   bias=epsb[:], scale=1.0 / 512))
                tr.op("act", [("rs2", sl)], [("rs2", sl)],
                      lambda: nc.scalar.activation(out=rs2[:, sl, :], in_=rs2[:, sl, :], func=AF.Exp, scale=-0.5))
                tr.op("dve", omk + [("rs2", sl), ("gout",)], [("onb", sl, 0)],
                      lambda: nc.vector.scalar_tensor_tensor(
                          out=onb[:, sl, 0:512], in0=om[:, j, :], scalar=rs2[:, sl, 0:1], in1=gout[:, 0:512],
                          op0=ALU.mult, op1=ALU.mult))
                tr.op("dve", osk + [("rs2", sl), ("gout",)], [("onb", sl, 1)],
                      lambda: nc.vector.scalar_tensor_tensor(
                          out=onb[:, sl, 512:1024], in0=osw[:, sl, :], scalar=rs2[:, sl, 1:2],
                          in1=gout[:, 512:1024], op0=ALU.mult, op1=ALU.mult))

            def swa_finish_b(j):
                sl = j % 2

                def trn():
                    inst = None
                    for c in range(8):
                        inst = nc.tensor.transpose(PTR[:, c * 128:(c + 1) * 128],
                                                   onb[:, sl, c * 128:(c + 1) * 128], ident[:])
                    return inst
                tr.op("pe", [("onb", sl, 0), ("onb", sl, 1), ("ident",)], [("PTR",)], trn)
                if j % 2 == 0:
                    tr.op("dve", [("PTR",)], [("uo", j)] + UK,
                          lambda: nc.vector.tensor_copy(out=u[:, :, j * 128:(j + 1) * 128],
                                                        in_=PTR[:].rearrange("p (c t) -> p c t", c=8)))
                else:
                    tr.op("act", [("PTR",)], [("uo", j)] + UK,
                          lambda: nc.scalar.copy(out=u[:, :, j * 128:(j + 1) * 128],
                                                 in_=PTR[:].rearrange("p (c t) -> p c t", c=8)))

            swa_scores(0)
            swa_scores(1)
            swa_pv(0)
            swa_norm(0)
            for j in range(4):
                if j + 2 < 4:
                    swa_scores(j + 2)
                if j + 1 < 4:
                    swa_pv(j + 1)
                    swa_norm(j + 1)
                swa_finish(j)
                if j >= 1:
                    swa_finish_b(j - 1)
            swa_finish_b(3)

            rst = ResidStats(n)
            for half in range(2):
                slot, skey = stA.get()
                wo = slot[:].rearrange("p (k d) -> p k d", k=8)
                for dd in range(4):
                    dc = half * 4 + dd
                    bank = next_bank()

                    def mm():
                        inst = None
                        for kc in range(8):
                            inst = nc.tensor.matmul(PS[bank][:], lhsT=wo[:, kc, dd * 128:(dd + 1) * 128],
                                                    rhs=u[:, kc, :], start=(kc == 0), stop=(kc == 7))
                        return inst
                    tr.op("pe", UK + [("uo", jj) for jj in range(4)] + [skey], [P(bank)], mm)
                    rst.after_pe_group()
                    tr.op("dve", [P(bank), ("h", n % 2, dc)], [("h", n % 2, dc)],
                          lambda: nc.vector.tensor_tensor(out=hcur[:, dc, :], in0=PS[bank][:], in1=hcur[:, dc, :],
                                                          op=ALU.add))
                    rst.chunk_done(dc)
                stA.release()
            rst.finish()

        NT = NSEQ * NTI
        h_stats(0)
        make_u(0, 0)
        for n in range(NT):
            si, ti = divmod(n, NTI)
            hcur = hTs[n % 2]
            if n + 1 < NT:
                load_x(n + 1)
            ffn(n)
            mixer(n)
            make_u(n, 16)

            ffn(n, next_n=(n + 1 if n + 1 < NT else None))
            for kc in range(8):
                tr.op("dve", [("h", n % 2, kc), ("rstd",), ("gains",)], [("h", n % 2, kc)],
                      lambda kc=kc: nc.vector.scalar_tensor_tensor(
                          out=hcur[:, kc, :], in0=hcur[:, kc, :], scalar=gains[:, 24 + kc:25 + kc], in1=rstd[:],
                          op0=ALU.mult, op1=ALU.mult))
            tr.dma("pool", f"xout{n % 2}", HKn(n), [],
                   outT[si, :, ti * T:(ti + 1) * T].rearrange("(kc p) t -> p kc t", p=128), hcur[:])
        for nm in ("xout0", "xout1"):
            if nm in tr.dma_sems:
                s = tr.dma_sems[nm]
                nc.gpsimd.wait_ge(s[0], s[1])
                nc.sync.wait_ge(s[0], s[1])
    return nc


def _t5_bucket(dist):
    n = np.maximum(dist, 0)
    max_exact = 16
    nf = np.maximum(n, 1).astype(np.float32)
    large = max_exact + (np.log(nf / max_exact) / math.log(128 / max_exact) * (32 - max_exact)).astype(np.int32)
    large = np.minimum(large, 31)
    return np.where(n < max_exact, n, large)


def _constants():
    k = np.arange(128)[:, None]
    q = np.arange(128)[None, :]
    ident = np.eye(128, dtype=np.float32)
    cmask = np.where(k > q, NEG, 0.0).astype(np.float32)
    dist_prev = q - k + 128
    dist_cur = q - k
    valid_prev = (dist_prev >= 0) & (dist_prev < 128)
    valid_cur = (dist_cur >= 0) & (dist_cur < 128)
    mi_prev = np.where(valid_prev, 0.0, NEG).astype(np.float32)
    mi_cur = np.where(valid_cur, 0.0, NEG).astype(np.float32)
    cst = np.concatenate([ident, cmask, mi_prev, mi_cur], axis=1)
    bp = _t5_bucket(dist_prev)
    bc = _t5_bucket(dist_cur)
    mb = np.zeros((128, 32, 2, 128), np.float32)
    for b in range(32):
        mb[:, b, 0, :] = ((bp == b) & valid_prev)
        mb[:, b, 1, :] = ((bc == b) & valid_cur)
    pos = np.arange(S, dtype=np.float32)
    inv_freq = (10000.0 ** (-np.arange(0, 32, 2, dtype=np.float32) / 32)).astype(np.float32)
    ang = pos[None, :] * inv_freq[:, None]
    cos = np.cos(ang).astype(np.float32)
    sin = np.sin(ang).astype(np.float32)
    cos_t = np.concatenate([cos, cos], axis=0)
    sin_t = np.concatenate([-sin, sin], axis=0)
    rope = np.zeros((NTI, 4, 32, 2, T), np.float32)
    for ti in range(NTI):
        rope[ti, :, :, 0, :] = cos_t[None, :, ti * T:(ti + 1) * T]
        rope[ti, :, :, 1, :] = sin_t[None, :, ti * T:(ti + 1) * T]
    return cst, mb.reshape(128, 32 * 256), rope.reshape(NTI, 128, 2 * T)


def _kc_layout(w):
    K, F = w.shape
    return np.ascontiguousarray(w.reshape(K // 128, 128, F).transpose(1, 0, 2))


def prepare_shared(inp):
    f32 = np.float32
    sh = {}
    for f, (gn, un, dn) in enumerate([("w_ffn1_gate", "w_ffn1_up", "w_ffn1_down"),
                                      ("w_ffn2_gate", "w_ffn2_up", "w_ffn2_down")]):
        wg = _kc_layout(np.asarray(inp[gn][0], f32))
        wu = _kc_layout(np.asarray(inp[un][0], f32))
        wgu = np.stack([wg.reshape(128, 8, 11, 256), wu.reshape(128, 8, 11, 256)], axis=0)
        wgu = np.ascontiguousarray(wgu.transpose(3, 1, 0, 2, 4)).reshape(11, 128, ASLOT)
        sh[f"wgu{f + 1}"] = wgu
        wd = np.asarray(inp[dn][0], f32).reshape(NFC, 128, 8, 128)
        wd = wd.transpose(2, 1, 0, 3)
        wdh = np.zeros((2, 8, 128, HCH, 128), f32)
        wdh[0] = wd[:, :, 0:12, :]
        wdh[1, :, :, 0:10, :] = wd[:, :, 12:22, :]
        sh[f"wd{f + 1}"] = np.ascontiguousarray(wdh).reshape(16, 128, BSLOT)
    w_in = np.asarray(inp["w_in"][0], f32)
    cq = w_in[:, 0:256]
    ckv = w_in[:, 256:384]
    kpe = w_in[:, 384:416]
    kpe_sw = np.concatenate([kpe[:, 16:32], kpe[:, 0:16]], axis=1)
    qs = w_in[:, 416:928]
    ks = w_in[:, 928:1056]
    vs = w_in[:, 1056:1184]
    g0 = _kc_layout(np.concatenate([cq, ckv, kpe, kpe_sw], axis=1)).reshape(128, 8 * 448)
    g1 = _kc_layout(qs).reshape(128, 8 * 512)
    g2 = _kc_layout(np.concatenate([ks, vs], axis=1)).reshape(128, 8 * 256)
    sh["win"] = np.ascontiguousarray(np.concatenate([g0, g1, g2], axis=1))
    wqb = np.asarray(inp["w_q_b"][0], f32).reshape(256, 8, 96)
    nope, pe = wqb[:, :, 0:64], wqb[:, :, 64:96]
    pe_sw = np.concatenate([pe[:, :, 16:32], pe[:, :, 0:16]], axis=2)
    wq = np.concatenate([nope.reshape(256, 512), pe.reshape(256, 256), pe_sw.reshape(256, 256)], axis=1)
    sh["wqb"] = _kc_layout(wq).reshape(128, 2 * 8 * 128)
    wkvb = np.asarray(inp["w_kv_b"][0], f32).reshape(128, 8, 128)
    sh["wkvb"] = np.ascontiguousarray(
        np.concatenate([wkvb[:, :, 0:64].reshape(128, 512), wkvb[:, :, 64:128].reshape(128, 512)], axis=1))
    wo = _kc_layout(np.asarray(inp["w_o"][0], f32))
    sh["wo"] = np.ascontiguousarray(wo.reshape(128, 8, 2, 512).transpose(2, 0, 1, 3)).reshape(2, 128, ASLOT)

    def gl(g):
        return np.asarray(g, f32).reshape(-1, 128).T
    sh["gains"] = np.ascontiguousarray(np.concatenate(
        [gl(inp["g_ffn1"][0]), gl(inp["g_mix"][0]), gl(inp["g_ffn2"][0]), gl(inp["g_final"]),
         gl(inp["g_q_a"][0]), gl(inp["g_kv_a"][0])], axis=1))
    sh["gout"] = np.ascontiguousarray(np.concatenate(
        [np.asarray(inp["g_out_mla"][0], f32), np.asarray(inp["g_out_swa"][0], f32)])[None, :])
    sh["sinks"] = np.ascontiguousarray(np.asarray(inp["attn_sinks"][0], f32)[None, :])
    sh["relb"] = np.ascontiguousarray(np.asarray(inp["rel_bias"], f32).reshape(1, 256))
    cst, mb, rope = _constants()
    sh["cst"], sh["mb"], sh["rope"] = cst, mb, rope
    return sh


_PROG_CACHE = {}


def kernel(**inputs):
    x = np.asarray(inputs["x"], np.float32)
    B = x.shape[0]
    ncores = 8
    nseq = B // ncores
    if nseq not in _PROG_CACHE:
        _PROG_CACHE[nseq] = build_program(nseq)
    nc = _PROG_CACHE[nseq]
    sh = prepare_shared(inputs)
    in_maps = []
    for c in range(ncores):
        m = dict(sh)
        m["xT"] = np.ascontiguousarray(x[c * nseq:(c + 1) * nseq].transpose(0, 2, 1))
        in_maps.append(m)
    res = run_bass_kernel_spmd(nc, in_maps, core_ids=list(range(ncores)))
    out = np.empty((B, S, D), np.float32)
    for c in range(ncores):
        out[c * nseq:(c + 1) * nseq] = res.results[c]["outT"].transpose(0, 2, 1)
    return out
```
